# Optimizing a Trainium2 kernel written in Bass

```python
import math
import jax, jax.numpy as jnp
from jax import lax
import numpy as np

D_MODEL = 2048
BATCH = 4
SEQ = 2048
DEPTH = 2

N_BRANCH = 4
BRANCH_WIDTH = D_MODEL // 4
GM_GROUPS = 4
GM_CHUNK = 128
GM_GROUP_DIM = BRANCH_WIDTH // GM_GROUPS
DA_HEADS = 4
DA_QK_DIM = BRANCH_WIDTH // (2 * DA_HEADS)
DA_V_DIM = 2 * DA_QK_DIM
DA_ROT_DIM = DA_QK_DIM // 4
FA_HEADS = 4
FA_HEAD_DIM = BRANCH_WIDTH // FA_HEADS
POOL_WINDOWS = (2, 4, 8, 16)
POOL_GROUPS = 4
POOL_GROUP_DIM = BRANCH_WIDTH // POOL_GROUPS
FFN_DIM = 4 * D_MODEL
ROPE_THETA = 500000.0
Q_BLOCK = 128
NORM_EPS = 1e-6

A_U = 0
A_V = A_U + BRANCH_WIDTH
B_Q = A_V + BRANCH_WIDTH
B_K = B_Q + 2 * DA_HEADS * DA_QK_DIM
B_V = B_K + 2 * DA_HEADS * DA_QK_DIM
C_Q = B_V + DA_HEADS * DA_V_DIM
C_K = C_Q + FA_HEADS * FA_HEAD_DIM
C_V = C_K + FA_HEADS * FA_HEAD_DIM
C_F = C_V + FA_HEADS * FA_HEAD_DIM
D_H = C_F + FA_HEADS
GATE = D_H + BRANCH_WIDTH
IN_COLS = GATE + N_BRANCH * D_MODEL

kernel_name = "hybrid_gated_parallel_mixers"


def rms_norm(x, g):
    xf = x.astype(jnp.float32)
    y = xf * lax.rsqrt(jnp.mean(xf * xf, axis=-1, keepdims=True) + NORM_EPS)
    return (y * g.astype(jnp.float32)).astype(x.dtype)


def rotary_tables(positions, rot_dim):
    inv = 1.0 / (ROPE_THETA ** (jnp.arange(0, rot_dim, 2, dtype=jnp.float32) / rot_dim))
    ang = positions.astype(jnp.float32)[..., None] * inv
    return jnp.cos(ang), jnp.sin(ang)


def apply_partial_rotary(x, cos, sin):
    half = cos.shape[-1]
    xf = x.astype(jnp.float32)
    x1, x2, xp = xf[..., :half], xf[..., half:2 * half], xf[..., 2 * half:]
    c, s = cos[:, :, None, :], sin[:, :, None, :]
    out = jnp.concatenate([x1 * c - x2 * s, x1 * s + x2 * c, xp], axis=-1)
    return out.astype(x.dtype)


def chunked_spatial_gating(u, v, ln_g, ln_b, w_s, b_s):
    B, S, _ = v.shape
    vf = v.astype(jnp.float32)
    mu = jnp.mean(vf, axis=-1, keepdims=True)
    var = jnp.mean(jnp.square(vf - mu), axis=-1, keepdims=True)
    vn = ((vf - mu) * lax.rsqrt(var + NORM_EPS) * ln_g.astype(jnp.float32)
          + ln_b.astype(jnp.float32)).astype(v.dtype)
    vn = vn.reshape(B, S // GM_CHUNK, GM_CHUNK, GM_GROUPS, GM_GROUP_DIM)
    causal = jnp.tril(jnp.ones((GM_CHUNK, GM_CHUNK), dtype=w_s.dtype))
    mixed = jnp.einsum('gts,bnsgc->bntgc', w_s * causal, vn) + b_s.T[:, :, None]
    return u * mixed.reshape(B, S, GM_GROUPS * GM_GROUP_DIM)


def differential_attention(q, k, v, lam, subln_g, lambda_init):
    B, S, H, _, dk = q.shape
    dv = v.shape[-1]
    nb = S // Q_BLOCK
    kh = k.transpose(0, 2, 3, 1, 4)
    vh = v.transpose(0, 2, 1, 3)
    q_blocks = q.transpose(0, 2, 3, 1, 4).reshape(B, H, 2, nb, Q_BLOCK, dk).transpose(3, 0, 1, 2, 4, 5)
    kpos = jnp.arange(S)
    scale = dk ** -0.5

    def one_block(args):
        qb, i = args
        qpos = i * Q_BLOCK + jnp.arange(Q_BLOCK)
        logits = jnp.einsum('bhmqd,bhmkd->bhmqk', qb, kh).astype(jnp.float32) * scale
        logits = jnp.where(qpos[:, None] >= kpos[None, :], logits, -jnp.inf)
        p = jax.nn.softmax(logits, axis=-1)
        w = p[:, :, 0] - lam * p[:, :, 1]
        return jnp.einsum('bhqk,bhkd->bhqd', w.astype(vh.dtype), vh)

    out = lax.map(one_block, (q_blocks, jnp.arange(nb)))
    out = out.transpose(1, 0, 3, 2, 4).reshape(B, S, H, dv)
    out = rms_norm(out, subln_g) * (1.0 - lambda_init)
    return out.reshape(B, S, H * dv)


def forgetting_attention(q, k, v, f_logit):
    B, S, H, d = q.shape
    nb = S // Q_BLOCK
    logf = jax.nn.log_sigmoid(f_logit.astype(jnp.float32))
    cum = jnp.cumsum(logf, axis=1).transpose(0, 2, 1)
    kh = k.transpose(0, 2, 1, 3)
    vh = v.transpose(0, 2, 1, 3)
    q_blocks = q.transpose(0, 2, 1, 3).reshape(B, H, nb, Q_BLOCK, d).transpose(2, 0, 1, 3, 4)
    c_blocks = cum.reshape(B, H, nb, Q_BLOCK).transpose(2, 0, 1, 3)
    kpos = jnp.arange(S)
    scale = d ** -0.5

    def one_block(args):
        qb, cq, i = args
        qpos = i * Q_BLOCK + jnp.arange(Q_BLOCK)
        logits = jnp.einsum('bhqd,bhkd->bhqk', qb, kh).astype(jnp.float32) * scale
        logits = logits + cq[..., None] - cum[:, :, None, :]
        logits = jnp.where(qpos[:, None] >= kpos[None, :], logits, -jnp.inf)
        p = jax.nn.softmax(logits, axis=-1)
        return jnp.einsum('bhqk,bhkd->bhqd', p.astype(vh.dtype), vh)

    out = lax.map(one_block, (q_blocks, c_blocks, jnp.arange(nb)))
    return out.transpose(1, 0, 3, 2, 4).reshape(B, S, H * d)


def multiscale_pool(h, w_pool, scale):
    B, S, _ = h.shape
    hg = h.reshape(B, S, POOL_GROUPS, POOL_GROUP_DIM)
    csum = jnp.cumsum(hg.astype(jnp.float32), axis=1)
    csum = jnp.concatenate([jnp.zeros_like(csum[:, :1]), csum], axis=1)
    win = jnp.array(POOL_WINDOWS, dtype=jnp.int32)
    t = jnp.arange(S, dtype=jnp.int32)[:, None]
    start = jnp.maximum(t + 1 - win[None, :], 0)
    gidx = jnp.arange(POOL_GROUPS)[None, :]
    total = csum[:, 1:] - csum[:, start, gidx]
    count = jnp.minimum(t + 1, win[None, :]).astype(jnp.float32)[None, :, :, None]
    pooled = (total / count - hg.astype(jnp.float32)).astype(h.dtype)
    y = jnp.einsum('bsgc,gcd->bsgd', pooled, w_pool)
    return y.reshape(B, S, POOL_GROUPS * POOL_GROUP_DIM) * scale


def setup_inputs(seed: int = 0) -> dict:
    key = jax.random.key(seed)
    ks = jax.random.split(key, 24)
    L = DEPTH
    f32 = jnp.float32

    def nrm(k, shape, s):
        return jax.random.normal(k, shape, f32) * s

    x = jax.random.normal(ks[0], (BATCH, SEQ, D_MODEL), f32)
    offs = jax.random.randint(ks[1], (BATCH, 1), 0, 4096, dtype=jnp.int32)
    positions = offs + jnp.arange(SEQ, dtype=jnp.int32)[None, :]
    return {
        "x": x,
        "positions": positions,
        "norm_mix_pre": 1.0 + nrm(ks[2], (L, D_MODEL), 0.05),
        "norm_mix_post": 1.0 + nrm(ks[3], (L, D_MODEL), 0.05),
        "norm_ffn_pre": 1.0 + nrm(ks[4], (L, D_MODEL), 0.05),
        "norm_ffn_post": 1.0 + nrm(ks[5], (L, D_MODEL), 0.05),
        "w_in": nrm(ks[6], (L, D_MODEL, IN_COLS), D_MODEL ** -0.5),
        "gm_ln_g": 1.0 + nrm(ks[7], (L, BRANCH_WIDTH), 0.05),
        "gm_ln_b": nrm(ks[8], (L, BRANCH_WIDTH), 0.02),
        "gm_w_s": nrm(ks[9], (L, GM_GROUPS, GM_CHUNK, GM_CHUNK), GM_CHUNK ** -0.5),
        "gm_b_s": 1.0 + nrm(ks[10], (L, GM_GROUPS, GM_CHUNK), 0.1),
        "da_lambda": nrm(ks[11], (L, 4, DA_QK_DIM), 0.1),
        "da_subln_g": 1.0 + nrm(ks[12], (L, DA_V_DIM), 0.05),
        "fa_b_f": 2.0 + nrm(ks[13], (L, FA_HEADS), 0.1),
        "pool_w": nrm(ks[14], (L, POOL_GROUPS, POOL_GROUP_DIM, POOL_GROUP_DIM), POOL_GROUP_DIM ** -0.5),
        "pool_scale": 1.0 + nrm(ks[15], (L, BRANCH_WIDTH), 0.1),
        "w_branch": nrm(ks[16], (L, N_BRANCH, BRANCH_WIDTH, D_MODEL), BRANCH_WIDTH ** -0.5),
        "w_out": nrm(ks[17], (L, D_MODEL, D_MODEL), D_MODEL ** -0.5),
        "w_ffn_up": nrm(ks[18], (L, D_MODEL, FFN_DIM), D_MODEL ** -0.5),
        "w_ffn_down": nrm(ks[19], (L, FFN_DIM, D_MODEL), FFN_DIM ** -0.5),
    }


def reference(x, positions, norm_mix_pre, norm_mix_post, norm_ffn_pre, norm_ffn_post,
              w_in, gm_ln_g, gm_ln_b, gm_w_s, gm_b_s, da_lambda, da_subln_g, fa_b_f,
              pool_w, pool_scale, w_branch, w_out, w_ffn_up, w_ffn_down):
    B, S, _ = x.shape
    cos, sin = rotary_tables(positions, DA_ROT_DIM)
    h = x
    for l in range(DEPTH):
        lambda_init = 0.8 - 0.6 * math.exp(-0.3 * l)
        xn = rms_norm(h, norm_mix_pre[l])
        proj = xn @ w_in[l]

        o_a = chunked_spatial_gating(proj[..., A_U:A_V], proj[..., A_V:B_Q],
                                     gm_ln_g[l], gm_ln_b[l], gm_w_s[l], gm_b_s[l])

        q_b = apply_partial_rotary(proj[..., B_Q:B_K].reshape(B, S, 2 * DA_HEADS, DA_QK_DIM), cos, sin)
        k_b = apply_partial_rotary(proj[..., B_K:B_V].reshape(B, S, 2 * DA_HEADS, DA_QK_DIM), cos, sin)
        v_b = proj[..., B_V:C_Q].reshape(B, S, DA_HEADS, DA_V_DIM)
        lp = da_lambda[l].astype(jnp.float32)
        lam = jnp.exp(jnp.sum(lp[0] * lp[1])) - jnp.exp(jnp.sum(lp[2] * lp[3])) + lambda_init
        o_b = differential_attention(q_b.reshape(B, S, DA_HEADS, 2, DA_QK_DIM),
                                     k_b.reshape(B, S, DA_HEADS, 2, DA_QK_DIM),
                                     v_b, lam, da_subln_g[l], lambda_init)

        q_c = proj[..., C_Q:C_K].reshape(B, S, FA_HEADS, FA_HEAD_DIM)
        k_c = proj[..., C_K:C_V].reshape(B, S, FA_HEADS, FA_HEAD_DIM)
        v_c = proj[..., C_V:C_F].reshape(B, S, FA_HEADS, FA_HEAD_DIM)
        f_logit = proj[..., C_F:D_H] + fa_b_f[l]
        o_c = forgetting_attention(q_c, k_c, v_c, f_logit)

        o_d = multiscale_pool(proj[..., D_H:GATE], pool_w[l], pool_scale[l])

        branches = jnp.stack([o_a, o_b, o_c, o_d], axis=2)
        gates = jax.nn.sigmoid(proj[..., GATE:].reshape(B, S, N_BRANCH, D_MODEL))
        branch_d = jnp.einsum('bsnc,ncd->bsnd', branches, w_branch[l])
        merged = jnp.einsum('bsnd,bsnd->bsd', gates, branch_d)
        h = h + rms_norm(merged @ w_out[l], norm_mix_post[l])

        hn = rms_norm(h, norm_ffn_pre[l])
        ff = jnp.square(jax.nn.relu(hn @ w_ffn_up[l])) @ w_ffn_down[l]
        h = h + rms_norm(ff, norm_ffn_post[l])
    return h
```

```python
import contextlib
import math
import numpy as np
import concourse.bass as bass
import concourse.mybir as mybir
from concourse.bass_utils import run_bass_kernel_spmd

F32 = mybir.dt.float32
BF16 = mybir.dt.bfloat16
I32 = mybir.dt.int32
AF = mybir.ActivationFunctionType
ALU = mybir.AluOpType
AX = mybir.AxisListType

D = 2048
T = 1024
NT = 8
KC = 16
IN_COLS = 12804
A_U, A_V, B_Q, B_K, B_V, C_Q, C_K, C_V, C_F, D_H, GATE = 0, 512, 1024, 1536, 2048, 2560, 3072, 3584, 4096, 4100, 4612
EPS = 1e-6
NEG = -30000.0


class Sched:
    ENGS = ['pe', 'act', 'dve', 'pool', 'sp']

    def __init__(self, nc, same_engine_sync=True):
        self.nc = nc
        self.ops = []
        self.last_w = {}
        self.readers = {}
        self.same_engine_sync = same_engine_sync

    def op(self, eng, fn, reads=(), writes=(), dma=None, inc=16, nobarrier=False):
        idx = len(self.ops)
        reads = list(reads)
        writes = list(writes)
        if not nobarrier:
            reads.append('PHASE')
        deps = set()
        for k in reads + writes:
            w = self.last_w.get(k)
            if w is not None:
                deps.add(w)
        for k in writes:
            for r in self.readers.get(k, {}).values():
                deps.add(r)
        self.ops.append(dict(eng=eng, fn=fn, deps=deps, dma=dma, sig=None, inc=inc))
        for k in writes:
            self.last_w[k] = idx
            self.readers[k] = {}
        rk = eng if dma is None else ('dma', idx)
        for k in reads:
            self.readers.setdefault(k, {})[rk] = idx
        return idx

    def _skip(self, o, d):
        if d['dma'] is not None or o['dma'] is not None:
            return False
        if d['eng'] == o['eng']:
            if d['eng'] == 'pe':
                return True
            return not self.same_engine_sync
        return False

    def emit(self, block, stack):
        nc = self.nc
        ops = self.ops
        needed = [False] * len(ops)
        for o in ops:
            for d in o['deps']:
                if not self._skip(o, ops[d]):
                    needed[d] = True
        cnt = {}
        for i, o in enumerate(ops):
            if o['fn'] is None:
                continue
            if o['dma'] is not None:
                key = ('dma', o['dma'])
                cnt[key] = cnt.get(key, 0) + o['inc']
                o['sig'] = (key, cnt[key])
            elif needed[i]:
                key = ('eng', o['eng'])
                cnt[key] = cnt.get(key, 0) + 1
                o['sig'] = (key, cnt[key])
        sems = {}
        for n, key in enumerate(cnt):
            sems[key] = stack.enter_context(nc.semaphore("s%d" % n))
        self.sem_counts = cnt
        per_eng = {e: [i for i, o in enumerate(ops) if o['eng'] == e] for e in self.ENGS}

        def run(engname, e):
            waited = {}
            for i in per_eng[engname]:
                o = ops[i]
                need = {}
                for d in o['deps']:
                    dd = ops[d]
                    if dd['sig'] is None or self._skip(o, dd):
                        continue
                    key, val = dd['sig']
                    if val > need.get(key, 0):
                        need[key] = val
                for key, val in need.items():
                    if waited.get(key, 0) >= val:
                        continue
                    e.wait_ge(sems[key], val)
                    waited[key] = val
                if o['fn'] is not None:
                    ins = o['fn'](e)
                    if o['sig'] is not None:
                        key, val = o['sig']
                        if o['dma'] is not None and o['inc'] == 1:
                            ins.then_inc(sems[key])
                        else:
                            ins.then_inc(sems[key], o['inc'] if o['dma'] is not None else 1)

        @block.tensor
        def _(e):
            run('pe', e)

        @block.scalar
        def _(e):
            run('act', e)

        @block.vector
        def _(e):
            run('dve', e)

        @block.gpsimd
        def _(e):
            run('pool', e)

        @block.sync
        def _(e):
            run('sp', e)


def build(depth=2, dbg=None, stop=99, nocc=False, dbg_layer=0, mode='fused', layer_base=0):
    nc = bass.Bass("TRN2", target_bir_lowering=False)

    def din(name, shape, dt=F32):
        return nc.dram_tensor(name, shape, dt, kind="ExternalInput").ap()

    x = din("x", [T, D])
    pos = din("pos", [128, NT], I32)
    cfg = din("cfg", [128, 74])
    consts = din("consts", [128, 5, 128])
    norm_mix_pre = din("norm_mix_pre", [2, D])
    norm_mix_post = din("norm_mix_post", [2, D])
    norm_ffn_pre = din("norm_ffn_pre", [2, D])
    norm_ffn_post = din("norm_ffn_post", [2, D])
    w_in = din("w_in", [2, D, IN_COLS])
    gm_ln_g = din("gm_ln_g", [2, 512])
    gm_ln_b = din("gm_ln_b", [2, 512])
    gm_w_s = din("gm_w_s", [2, 4, 128, 128])
    gm_b_s = din("gm_b_s", [2, 4, 128])
    da_lambda = din("da_lambda", [2, 4, 64])
    fa_b_f = din("fa_b_f", [2, 4])
    pool_w = din("pool_w", [2, 4, 128, 128])
    pvec = din("pvec", [2, 128, 5])
    w_branch = din("w_branch", [2, 4, 512, D])
    w_out = din("w_out", [2, D, D])
    w_ffn_up = din("w_ffn_up", [2, D, 4 * D])
    w_ffn_down = din("w_ffn_down", [2, 4 * D, D])
    if mode == 'A':
        stop = 1
        nocc = True
    if mode == 'B':
        nocc = True
    out = None if mode == 'A' else nc.dram_tensor("out", [T, D], F32, kind="ExternalOutput").ap()
    hbuf = nc.dram_tensor("hbuf", [T, D], F32).ap()
    if mode == 'A':
        cc1i = [nc.dram_tensor("c1i_out", [2048, 1024], BF16, kind="ExternalOutput")] * depth
        cc2i = [nc.dram_tensor("c2i_out", [12, 1024], F32, kind="ExternalOutput")] * depth
    else:
        cc1i = [nc.dram_tensor("cc1i", [2048, 1024], BF16)] * depth
        cc2i = [nc.dram_tensor("cc2i", [12, 1024], F32)] * depth
    if mode == 'B':
        prev1 = nc.dram_tensor("prev1", [2048, 1024], BF16, kind="ExternalInput")
        prev2 = nc.dram_tensor("prev2", [24, 1024], F32, kind="ExternalInput")
        cc1o = None
        cc2o = [prev2] * depth
    else:
        cc1o = [[nc.dram_tensor("cc1o_%d" % r, [512, 1024], BF16) for r in range(8)]] * depth
        cc2o = [nc.dram_tensor("cc2o", [24, 1024], F32)] * depth
    dbg_out = {}
    if dbg:
        for name, shape in dbg.items():
            dbg_out[name] = nc.dram_tensor("dbg_" + name, shape, F32, kind="ExternalOutput").ap()
    RG = [[0, 1], [2, 3], [4, 5], [6, 7]]

    with contextlib.ExitStack() as st:
        def sb(name, shape, dt):
            return st.enter_context(nc.sbuf_tensor(name, shape, dt))

        A1 = sb("A1", [128, 32768], BF16)
        A2 = sb("A2", [128, 16384], BF16)
        X = sb("X", [128, 16384], BF16)
        wbuf = [sb("wbuf%d" % i, [128, KC, 512], BF16) for i in range(2)]
        wbr = [sb("wbr%d" % i, [128, 4, 512], BF16) for i in range(2)]
        gbc = sb("gbc", [128, D], F32)
        QB = sb("QB", [128, 4, T], BF16)
        QC = sb("QC", [128, 4, T], BF16)
        cst = sb("cst", [128, 5, 128], F32)
        identb = sb("identb", [128, 128], BF16)
        maskb = sb("maskb", [128, 128], BF16)
        onesb = sb("onesb", [128, 128], BF16)
        cfgs = sb("cfgs", [128, 74], F32)
        posi = sb("posi", [128, NT], I32)
        posf = sb("posf", [128, NT], F32)
        ang = sb("ang", [128, NT, 8], F32)
        cosT = sb("cosT", [128, NT, 8], F32)
        sinT = sb("sinT", [128, NT, 8], F32)
        stat = sb("stat", [128, 64], F32)
        ssq = sb("ssq", [128, NT, 4], F32)
        stg = [sb("stg%d" % i, [128, 512], F32) for i in range(2)]
        stgb = [sb("stgb%d" % i, [128, 512], BF16) for i in range(2)]
        rt = sb("rt", [128, 4, 8, 8], F32)
        pv = sb("pv", [128, 5], F32)
        lam = sb("lam", [128, 8], F32)
        bfb = sb("bfb", [128, 4], F32)
        wf = sb("wf", [128, KC, 4], BF16)
        scr = sb("scr", [128, 8], F32)
        epsc = sb("epsc", [128, 1], F32)
        kint = sb("kint", [128, NT, 8], I32)
        pa = [st.enter_context(nc.psum_tensor("pa%d" % i, [128, 512], F32)) for i in range(6)]
        pt = [st.enter_context(nc.psum_tensor("pt%d" % i, [128, 1024], BF16)) for i in range(2)]
        block = st.enter_context(nc.Block())
        S = Sched(nc)

        xnT = A1[:, 0:16384].rearrange("p (k t) -> p k t", k=KC)
        oT = A1[:, 16384:32768].rearrange("p (c t) -> p c t", c=16)
        yv = A1[:, :].bitcast(F32).rearrange("p (i c) -> p i c", i=NT)
        merged = A2[:, :].rearrange("p (k t) -> p k t", k=KC)
        hnT = merged
        Xf = X[:, :].bitcast(F32)
        A2f = A2[:, :].bitcast(F32)
        htb = [Xf[:, 0:2048], Xf[:, 2048:4096]]
        xb = X[:, 8192:10240]
        ident_f = cst[:, 0, :]
        triu_f = cst[:, 1, :]
        tril_f = cst[:, 2, :]
        ones_f = cst[:, 3, :]
        half_f = cst[:, 4, :]
        prevb = cfgs[:, 0:1]
        flag = cfgs[:, 1:2]
        rcnt = cfgs[:, 2:66].rearrange("p (g t) -> p g t", g=4)
        invf = cfgs[:, 66:74]

        cnts = {'pa': 0, 'w': 0, 'stg': 0, 'stgb': 0, 'ht': 0, 'wbr': 0}

        def nxt(kind, n):
            v = cnts[kind] % n
            cnts[kind] += 1
            return v

        def mm(o, lhsT, rhs, start, stop, reads, writes):
            S.op('pe', lambda e: e.matmul(o, lhsT=lhsT, rhs=rhs, start=start, stop=stop), reads=reads, writes=writes)

        def tp(o, in_, ident, reads, writes):
            S.op('pe', lambda e: e.transpose(out=o, in_=in_, identity=ident), reads=reads, writes=writes)

        def act(o, in_, func, reads, writes, bias=None, scale=None, accum=None):
            kw = {}
            if bias is not None:
                kw['bias'] = bias
            if scale is not None:
                kw['scale'] = scale
            if accum is not None:
                kw['accum_out'] = accum
            S.op('act', lambda e: e.activation(out=o, in_=in_, func=func, **kw), reads=reads, writes=writes)

        def tt(eng, o, a, b, op, reads, writes):
            S.op(eng, lambda e: e.tensor_tensor(out=o, in0=a, in1=b, op=op), reads=reads, writes=writes)

        def ts(eng, o, a, s1, s2, op0, op1, reads, writes):
            if s2 is None:
                S.op(eng, lambda e: e.tensor_scalar(out=o, in0=a, scalar1=s1, scalar2=None, op0=op0), reads=reads, writes=writes)
            else:
                S.op(eng, lambda e: e.tensor_scalar(out=o, in0=a, scalar1=s1, scalar2=s2, op0=op0, op1=op1), reads=reads, writes=writes)

        def stt(eng, o, a, s, b, op0, op1, reads, writes):
            S.op(eng, lambda e: e.scalar_tensor_tensor(out=o, in0=a, scalar=s, in1=b, op0=op0, op1=op1), reads=reads, writes=writes)

        def cp(eng, o, a, reads, writes):
            S.op(eng, lambda e: e.tensor_copy(out=o, in_=a), reads=reads, writes=writes)

        def dma(q, o, in_, reads, writes, key, nobarrier=False):
            S.op(q, lambda e: e.dma_start(out=o, in_=in_), reads=reads, writes=writes, dma=key, nobarrier=nobarrier)

        def barrier():
            S.op('dve', lambda e: e.memset(scr[:, 0:1], 0.0), reads=[], writes=['PHASE', 'scr'])

        def rstd_from(o, a, scale, reads, writes):
            act(o, a, AF.Sqrt, reads, writes, bias=epsc[:, 0:1], scale=scale)
            S.op('dve', lambda e: e.reciprocal(out=o, in_=o), reads=writes, writes=writes)

        def load_w(src, ncols=512):
            b = nxt('w', 2)
            kk = src.shape[0] // 128
            dma('pool', wbuf[b][:, 0:kk, 0:ncols], src.rearrange("(k p) c -> p k c", p=128), [], [('w', b)], ('w', b), nobarrier=True)
            return b

        def dump(name, src_ap, reads):
            if name in dbg_out:
                dma('pool', dbg_out[name], src_ap, reads, [], ('dbg', name))

        dma('sp', cst[:], consts, [], ['cst'], 'cst')
        dma('sp', cfgs[:], cfg, [], ['cfgs'], 'cfgs')
        dma('sp', posi[:], pos, [], ['posi'], 'posi')
        cp('dve', identb[:], ident_f, ['cst'], ['identb'])
        cp('dve', maskb[:], triu_f, ['cst'], ['maskb'])
        cp('dve', onesb[:], ones_f, ['cst'], ['onesb'])
        cp('dve', posf[:], posi[:], ['posi'], ['posf'])
        for i in range(NT):
            ts('dve', ang[:, i, :], invf, posf[:, i:i + 1], None, ALU.mult, None, ['posf', 'cfgs'], ['ang'])
        S.op('dve', lambda e: e.memset(epsc[:], EPS), reads=[], writes=['epsc'])

        def sin_of(dst, shift):
            a2 = rt[:, 0:2].rearrange("p a i j -> p (a i) j")[:, 0:8, :]
            kf = rt[:, 2:4].rearrange("p a i j -> p (a i) j")[:, 0:8, :]
            ts('dve', a2, ang[:], shift, None, ALU.add, None, ['ang'], ['rt'])
            ts('dve', kf, a2, 1.0 / (2 * math.pi), None, ALU.mult, None, ['rt'], ['rt'])
            cp('dve', kint[:], kf, ['rt'], ['kint'])
            cp('dve', kf, kint[:], ['kint'], ['rt'])
            stt('dve', a2, kf, -6.28125, a2, ALU.mult, ALU.add, ['rt'], ['rt'])
            stt('dve', a2, kf, -(2 * math.pi - 6.28125), a2, ALU.mult, ALU.add, ['rt'], ['rt'])
            ts('dve', kf, a2, math.pi, -2 * math.pi, ALU.is_gt, ALU.mult, ['rt'], ['rt'])
            tt('dve', a2, a2, kf, ALU.add, ['rt'], ['rt'])
            ts('dve', kf, a2, -math.pi, 2 * math.pi, ALU.is_lt, ALU.mult, ['rt'], ['rt'])
            tt('dve', a2, a2, kf, ALU.add, ['rt'], ['rt'])
            ts('dve', a2, a2, math.pi, -math.pi, ALU.min, ALU.max, ['rt'], ['rt'])
            act(dst, a2, AF.Sin, ['rt'], [dst.name if False else 'trig'])

        sin_of(sinT[:], 0.0)
        sin_of(cosT[:], 0.5 * math.pi)

        def norm_transpose(src, gain, dstT):
            dma('sp', gbc[:], gain.partition_broadcast(128), [], ['gbc'], 'gbc')
            for i in range(NT):
                hb = nxt('ht', 2)
                ht = htb[hb]
                dma('sp', ht, src[i * 128:(i + 1) * 128, :], [('hbuf', i)], [('ht', hb)], ('ht', hb))
                act(xb, ht, AF.Square, [('ht', hb)], ['xb', 'stat'], accum=stat[:, 0:1])
                rstd_from(stat[:, 1:2], stat[:, 0:1], 1.0 / D, ['stat'], ['stat'])
                stt('dve', xb, ht, stat[:, 1:2], gbc[:], ALU.mult, ALU.mult, [('ht', hb), 'stat', 'gbc'], ['xb'])
                for hf in range(2):
                    for k in range(8):
                        kk = hf * 8 + k
                        tp(pt[hf][:, k * 128:(k + 1) * 128], xb[:, kk * 128:(kk + 1) * 128], identb[:], ['xb', 'identb'], [('pt', hf)])
                    src_v = pt[hf][:, :].rearrange("p (k t) -> p k t", k=8)
                    dst_v = dstT[:, hf * 8:(hf + 1) * 8, i * 128:(i + 1) * 128]
                    if hf == 0:
                        act(dst_v, src_v, AF.Copy, [('pt', hf)], [('dstT', i)])
                    else:
                        cp('dve', dst_v, src_v, [('pt', hf)], [('dstT', i)])

        def proj_tok(col0, ncols, evac, wsrc=None, lhs=None, lhs_key='dstT'):
            src = w_in_l[:, col0:col0 + ncols] if wsrc is None else wsrc
            b = load_w(src, ncols)
            L = xnT if lhs is None else lhs
            for i in range(NT):
                p = nxt('pa', 6)
                for k in range(KC):
                    mm(pa[p][:, 0:ncols], L[:, k, i * 128:(i + 1) * 128], wbuf[b][:, k, 0:ncols], k == 0, k == KC - 1,
                       [(lhs_key, i), ('w', b)], [('pa', p)])
                evac(i, pa[p][:, 0:ncols], ('pa', p))

        def proj_feat(col0, evac, wsrc=None, rhsT=None, nchunk=4):
            src = w_in_l[:, col0:col0 + 128 * nchunk] if wsrc is None else wsrc
            b = load_w(src, 128 * nchunk)
            R = xnT if rhsT is None else rhsT
            for c in range(nchunk):
                for hf in range(2):
                    p = nxt('pa', 6)
                    for k in range(KC):
                        mm(pa[p][:], wbuf[b][:, k, c * 128:(c + 1) * 128], R[:, k, hf * 512:(hf + 1) * 512], k == 0, k == KC - 1,
                           [('dstT', i) for i in range(hf * 4, hf * 4 + 4)] + [('w', b)], [('pa', p)])
                    evac(c, hf, pa[p][:], ('pa', p))

        for l in range(depth):
            lw = layer_base + l
            w_in_l = w_in[lw]
            lam_init = 0.8 - 0.6 * math.exp(-0.3 * lw)
            hsrc = x if l == 0 else hbuf
            c1i, c2i, c2o = cc1i[l].ap(), cc2i[l].ap(), cc2o[l].ap()
            c1oc = [prev1.ap()[r * 256:(r + 1) * 256, :] for r in range(8)] if mode == 'B' else [t.ap() for t in cc1o[l]]
            barrier()
            dma('sp', pv[:], pvec[lw], [], ['pv'], 'pv')
            lpb = stg[1]
            dma('sp', lpb[:, 0:256], da_lambda[lw].rearrange("a d -> (a d)").partition_broadcast(128), [], [('stg', 1)], 'lpb')
            dma('sp', bfb[:], fa_b_f[lw].partition_broadcast(128), [], ['bfb'], 'bfb')
            lv = lpb[:, 0:256].rearrange("p (a b d) -> p a b d", a=2, b=2)
            lprod = stg[0][:, 0:128].rearrange("p (a d) -> p a d", a=2)
            tt('dve', lprod, lv[:, :, 0, :], lv[:, :, 1, :], ALU.mult, [('stg', 1)], [('stg', 0)])
            S.op('dve', lambda e: e.reduce_sum(out=lam[:, 0:2], in_=lprod, axis=AX.X),
                 reads=[('stg', 0)], writes=['lam'])
            act(lam[:, 2:4], lam[:, 0:2], AF.Exp, ['lam'], ['lam'])
            tt('dve', lam[:, 4:5], lam[:, 3:4], lam[:, 2:3], ALU.subtract, ['lam'], ['lam'])
            ts('dve', lam[:, 4:5], lam[:, 4:5], -lam_init, None, ALU.add, None, ['lam'], ['lam'])
            ts('dve', lam[:, 5:6], pv[:, 4:5], 1.0 - lam_init, None, ALU.mult, None, ['pv'], ['lam'])
            neglam = lam[:, 4:5]
            subg = lam[:, 5:6]

            norm_transpose(hsrc, norm_mix_pre[lw], xnT)
            barrier()
            if l == dbg_layer:
                dump('xnT', xnT[:, 0, :], [('dstT', i) for i in range(NT)])

            kTst = X[:, 0:4096].rearrange("p (a t) -> p a t", a=4)
            vst = X[:, 4096:8192].rearrange("p (i c) -> p i c", i=NT)
            hD = A2f[:, 0:4160].rearrange("p (g t) -> p g t", g=4)
            nlst = A2f[:, 4160:4192].rearrange("p (i h) -> p i h", i=NT)

            def qk_evac(which):
                def f(i, ps, pkey):
                    s = nxt('stg', 2)
                    g = stg[s]
                    act(g[:], ps, AF.Copy, [pkey], [('stg', s)])
                    gv = g[:, :].rearrange("p (m d) -> p m d", m=8)
                    x1 = gv[:, :, 0:8]
                    x2 = gv[:, :, 8:16]
                    cb = cosT[:, i:i + 1, :].to_broadcast([128, 8, 8])
                    sn = sinT[:, i:i + 1, :].to_broadcast([128, 8, 8])
                    tt('dve', rt[:, 0], x1, cb, ALU.mult, [('stg', s), 'trig'], ['rt'])
                    tt('dve', rt[:, 1], x2, sn, ALU.mult, [('stg', s), 'trig'], ['rt'])
                    tt('dve', rt[:, 2], x1, sn, ALU.mult, [('stg', s), 'trig'], ['rt'])
                    tt('dve', rt[:, 3], x2, cb, ALU.mult, [('stg', s), 'trig'], ['rt'])
                    tt('dve', x1, rt[:, 0], rt[:, 1], ALU.subtract, ['rt'], [('stg', s)])
                    tt('dve', x2, rt[:, 2], rt[:, 3], ALU.add, ['rt'], [('stg', s)])
                    sbi = nxt('stgb', 2)
                    act(stgb[sbi][:], g[:], AF.Copy, [('stg', s)], [('stgb', sbi)])
                    for a in range(4):
                        tp(pt[0][:, a * 128:(a + 1) * 128], stgb[sbi][:, a * 128:(a + 1) * 128], identb[:], [('stgb', sbi), 'identb'], [('pt', 0)])
                    srcv = pt[0][:, 0:512].rearrange("p (a t) -> p a t", a=4)
                    if which == 'q':
                        cp('dve', QB[:, :, i * 128:(i + 1) * 128], srcv, [('pt', 0)], ['QB'])
                    else:
                        cp('dve', kTst[:, :, i * 128:(i + 1) * 128], srcv, [('pt', 0)], ['kTst'])
                return f

            def v_evac(i, ps, pkey):
                act(vst[:, i, :], ps, AF.Copy, [pkey], ['vst'])

            proj_tok(B_Q, 512, qk_evac('q'))
            proj_tok(B_K, 512, qk_evac('k'))
            dma('sp', c1i[0:512, :].rearrange("(a p) t -> p a t", p=128), kTst, ['kTst'], ['c1i'], 'c1w0')
            proj_tok(B_V, 512, v_evac)
            dma('sp', c1i[1024:1536, :].rearrange("r (two c) -> (r two) c", two=2).rearrange("(i p) c -> p i c", p=128), vst, ['vst'], ['c1i'], 'c1w1')
            proj_feat(C_Q, lambda c, hf, ps, pkey: act(QC[:, c, hf * 512:(hf + 1) * 512], ps, AF.Copy, [pkey], ['QC']))
            proj_feat(C_K, lambda c, hf, ps, pkey: act(kTst[:, c, hf * 512:(hf + 1) * 512], ps, AF.Copy, [pkey], ['kTst']))
            dma('sp', c1i[512:1024, :].rearrange("(a p) t -> p a t", p=128), kTst, ['kTst'], ['c1i'], 'c1w0')
            proj_tok(C_V, 512, v_evac)
            dma('sp', c1i[1536:2048, :].rearrange("r (two c) -> (r two) c", two=2).rearrange("(i p) c -> p i c", p=128), vst, ['vst'], ['c1i'], 'c1w1')
            if not nocc:
                for r in range(8):
                    S.op('pool', lambda e, r=r: e.collective_compute("AllGather", ALU.bypass, replica_groups=RG,
                                                                     ins=[c1i[r * 256:(r + 1) * 256, :].opt()], outs=[c1oc[r].opt()]),
                         reads=['c1i'], writes=[('c1o', r)], dma=('cc1', r), inc=1)
            dma('pool', wf[:], w_in_l[:, C_F:C_F + 4].rearrange("(k p) c -> p k c", p=128), [], ['wf'], 'wf')
            for i in range(NT):
                p = nxt('pa', 6)
                for k in range(KC):
                    mm(pa[p][:, 0:4], xnT[:, k, i * 128:(i + 1) * 128], wf[:, k, :], k == 0, k == KC - 1, [('dstT', i), 'wf'], [('pa', p)])
                tt('dve', nlst[:, i, :], pa[p][:, 0:4], bfb[:], ALU.add, [('pa', p), 'bfb'], ['nlst'])
            act(nlst, nlst, AF.Exp, ['nlst'], ['nlst'], scale=-1.0)
            act(nlst, nlst, AF.Ln, ['nlst'], ['nlst'], bias=1.0)
            dma('sp', c2i[0:4, :].rearrange("r (t h) -> (r t) h", h=4).rearrange("(i p) h -> p i h", p=128), nlst, ['nlst'], ['c2i'], 'c2w0')
            proj_feat(D_H, lambda c, hf, ps, pkey: act(hD[:, c, 16 + hf * 512:16 + (hf + 1) * 512], ps, AF.Copy, [pkey], ['hD']))
            dma('sp', c2i[4:12, :].rearrange("r (c t) -> (r c) t", t=16).rearrange("(g c) t -> c g t", c=128), hD[:, :, 1024:1040], ['hD'], ['c2i'], 'c2w1')
            if not nocc:
                S.op('pool', lambda e: e.collective_compute("AllGather", ALU.bypass, replica_groups=RG, ins=[c2i.opt()], outs=[c2o.opt()]),
                     reads=['c2i'], writes=['c2o'], dma='cc2', inc=1)

            if stop < 2:
                break
            barrier()
            uT = X[:, 0:4096].rearrange("p (g t) -> p g t", g=4)
            vn = X[:, 4096:8192].rearrange("p (i c) -> p i c", i=NT)
            lng = Xf[:, 4096:4608]
            lnb = Xf[:, 4608:5120]
            bsb = Xf[:, 5120:5632].rearrange("p (g t) -> p g t", g=4)
            wsf = Xf[:, 5632:6144].rearrange("p (g t) -> p g t", g=4)
            WcT = X[:, 12288:12800].rearrange("p (g t) -> p g t", g=4)
            tmpA = Xf[:, 6400:6912].rearrange("p (g t) -> p g t", g=4)
            dma('sp', lng, gm_ln_g[lw].partition_broadcast(128), [], ['lng'], 'lng')
            dma('sp', lnb, gm_ln_b[lw].partition_broadcast(128), [], ['lnb'], 'lnb')
            dma('sp', bsb, gm_b_s[lw].rearrange("g t -> (g t)").partition_broadcast(128).rearrange("p (g t) -> p g t", g=4), [], ['bsb'], 'bsb')
            dma('sp', wsf, gm_w_s[lw].rearrange("g t s -> t g s"), [], ['wsf'], 'wsf')
            for g in range(4):
                tt('dve', wsf[:, g, :], wsf[:, g, :], tril_f, ALU.mult, ['wsf', 'cst'], ['wsf'])
            pw = nxt('pa', 6)
            for g in range(4):
                tp(pa[pw][:, g * 128:(g + 1) * 128], wsf[:, g, :], ident_f, ['wsf', 'cst'], [('pa', pw)])
            cp('dve', WcT, pa[pw][:, :].rearrange("p (g t) -> p g t", g=4), [('pa', pw)], ['WcT'])
            proj_feat(A_U, lambda c, hf, ps, pkey: act(uT[:, c, hf * 512:(hf + 1) * 512], ps, AF.Copy, [pkey], ['uT']))

            def av_evac(i, ps, pkey):
                s = nxt('stg', 2)
                sbi = nxt('stgb', 2)
                act(stgb[sbi][:], ps, AF.Square, [pkey], [('stgb', sbi), 'stat'], accum=stat[:, 8:9])
                S.op('dve', lambda e: e.reduce_sum(out=stat[:, 9:10], in_=ps, axis=AX.X), reads=[pkey], writes=['stat'])
                ts('dve', stat[:, 9:10], stat[:, 9:10], 1.0 / 512, None, ALU.mult, None, ['stat'], ['stat'])
                tt('dve', stat[:, 10:11], stat[:, 9:10], stat[:, 9:10], ALU.mult, ['stat'], ['stat'])
                stt('dve', stat[:, 11:12], stat[:, 8:9], 1.0 / 512, stat[:, 10:11], ALU.mult, ALU.subtract, ['stat'], ['stat'])
                rstd_from(stat[:, 12:13], stat[:, 11:12], 1.0, ['stat'], ['stat'])
                stt('dve', stat[:, 13:14], stat[:, 9:10], -1.0, stat[:, 12:13], ALU.mult, ALU.mult, ['stat'], ['stat'])
                act(stg[s][:], ps, AF.Identity, [pkey, 'stat'], [('stg', s)], bias=stat[:, 13:14], scale=stat[:, 12:13])
                tt('dve', stg[s][:], stg[s][:], lng, ALU.mult, [('stg', s), 'lng'], [('stg', s)])
                tt('dve', vn[:, i, :], stg[s][:], lnb, ALU.add, [('stg', s), 'lnb'], [('vn', i)])

            proj_tok(A_V, 512, av_evac)
            for i in range(NT):
                p = nxt('pa', 6)
                for g in range(4):
                    mm(pa[p][:, g * 128:(g + 1) * 128], vn[:, i, g * 128:(g + 1) * 128], WcT[:, g, :], True, True, [('vn', i), 'WcT'], [('pa', p)])
                tt('dve', tmpA, pa[p][:, :].rearrange("p (g t) -> p g t", g=4), bsb, ALU.add, [('pa', p), 'bsb'], ['tmpA'])
                tt('dve', oT[:, 0:4, i * 128:(i + 1) * 128], tmpA, uT[:, :, i * 128:(i + 1) * 128], ALU.mult, ['tmpA', 'uT'], [('oT', 0)])
            if l == dbg_layer:
                dump('oA', oT[:, 0, :], [('oT', 0)])

            if stop < 3:
                break
            barrier()
            halo = Xf[:, 0:64].rearrange("p (g t) -> p g t", g=4)
            ta = Xf[:, 64:1104]
            tb = Xf[:, 1104:2144]
            pT = X[:, 8192:9216]
            pwb = X[:, 9216:9728].rearrange("p (g d) -> p g d", g=4)
            dma('sp', halo, c2o[4:12, :].rearrange("r (c t) -> (r c) t", t=16).rearrange("(g c) t -> c g t", c=128), ['c2o'], ['halo'], 'halo')
            dma('pool', pwb, pool_w[lw].rearrange("g c d -> c g d"), [], ['pwb'], 'pwb')
            ts('dve', hD[:, :, 0:16], halo, flag, None, ALU.mult, None, ['halo', 'cfgs', 'hD'], ['hD'])
            for g in range(4):
                a = hD[:, g, :]
                cur = a
                bufs = [ta, tb]
                for step in range(g + 1):
                    sh = 1 << step
                    dst = bufs[step % 2]
                    eng = 'dve' if step % 2 == 0 else 'pool'
                    tt(eng, dst[:, sh:1040], cur[:, sh:1040], cur[:, 0:1040 - sh], ALU.add, ['hD', 'ta', 'tb'], ['ta' if step % 2 == 0 else 'tb'])
                    cur = dst
                wdt = 2 << g
                stt('dve', pT[:, :], cur[:, 16:1040], 1.0 / wdt, a[:, 16:1040], ALU.mult, ALU.subtract, ['ta', 'tb', 'hD'], ['pT'])
                tt('dve', halo[:, g, :], cur[:, 16:32], rcnt[:, g, :], ALU.mult, ['ta', 'tb', 'cfgs'], ['halo'])
                tt('dve', pT[:, 0:16], halo[:, g, :], a[:, 16:32], ALU.subtract, ['halo', 'hD'], ['pT'])
                for hf in range(2):
                    p = nxt('pa', 6)
                    mm(pa[p][:], pwb[:, g, :], pT[:, hf * 512:(hf + 1) * 512], True, True, ['pT', 'pwb'], [('pa', p)])
                    act(oT[:, 12 + g, hf * 512:(hf + 1) * 512], pa[p][:], AF.Copy, [('pa', p), 'pv'], [('oT', 3)], scale=pv[:, g:g + 1])
            if l == dbg_layer:
                dump('oD', oT[:, 12, :], [('oT', 3)])

            if stop < 4:
                break
            barrier()
            kTh = [X[:, 0:2048], X[:, 2048:4096]]
            vh = [X[:, 4096:6144].rearrange("p (j c) -> p j c", j=16), X[:, 6144:8192].rearrange("p (j c) -> p j c", j=16)]
            Pb = [X[:, 8192:8704], X[:, 8704:9216]]
            r1 = Xf[:, 4608:5120]
            r2 = Xf[:, 5120:5632]
            oa = Xf[:, 5632:6144]
            ob = Xf[:, 6144:6656]
            sqb = X[:, 13312:13824]
            nl_all = Xf[:, 7168:7232].rearrange("p (j h) -> p j h", j=16)
            cum = Xf[:, 7232:7296].rearrange("p (j h) -> p j h", j=16)
            pref = Xf[:, 7296:7360].rearrange("p (j h) -> p j h", j=16)
            tot = Xf[:, 7360:7424].rearrange("p (j h) -> p j h", j=16)
            crefn = Xf[:, 7424:7456].rearrange("p (q h) -> p q h", q=8)
            btab = Xf[:, 7456:7968].rearrange("p (h j q) -> p h j q", h=4, j=16)

            def load_kv(hd, krow0, vrow0, slot):
                kr = krow0 + hd * 128
                dma('sp', kTh[slot][:, 0:1024], c1oc[kr // 256][kr % 256:kr % 256 + 128, :], [('c1o', kr // 256)], [('kTh', slot)], ('kTh', slot))
                dma('sp', kTh[slot][:, 1024:2048], c1i[kr:kr + 128, :], ['c1i'], [('kTh', slot)], ('kTh', slot))
                for hh in range(2):
                    vsrc = c1oc[vrow0 // 256 + hh][0:256, :].rearrange("r (two c) -> (r two) c", two=2)[:, hd * 128:(hd + 1) * 128]
                    dma('sp', vh[slot][:, hh * 4:hh * 4 + 4, :], vsrc.rearrange("(j q) c -> q j c", q=128), [('c1o', vrow0 // 256 + hh)], [('vh', slot)], ('vh', slot))
                vsrc = c1i[vrow0:vrow0 + 512, :].rearrange("r (two c) -> (r two) c", two=2)[:, hd * 128:(hd + 1) * 128]
                dma('sp', vh[slot][:, 8:16, :], vsrc.rearrange("(j q) c -> q j c", q=128), ['c1i'], [('vh', slot)], ('vh', slot))

            def attention(nmaps, Q, krow0, vrow0, finalize, bias_fn, scale):
                dk = 128 // nmaps
                for hd in range(4):
                    slot = hd % 2
                    load_kv(hd, krow0, vrow0, slot)
                    for qc in range(2):
                        nj = 8 + 4 * qc + 4
                        for j in range(nj):
                            lo = 0 if j < 8 + 4 * qc else (j - 8 - 4 * qc) * 128
                            diag = j >= 8 + 4 * qc
                            for m in range(nmaps):
                                psS = 4 + nxt('pa', 2)
                                mm(pa[psS][:, lo:512], kTh[slot][m * dk:(m + 1) * dk, j * 128:(j + 1) * 128],
                                   Q[m * dk:(m + 1) * dk, hd, qc * 512 + lo:(qc + 1) * 512], True, True,
                                   [('kTh', slot)], [('pa', psS)])
                                pbi = nxt('stgb', 2)
                                P = Pb[pbi]
                                bias_fn(P, pa[psS], ('pa', psS), ('Pb', pbi), hd, j, qc, lo, scale)
                                if diag:
                                    tt('pool', P[:, lo:lo + 128], P[:, lo:lo + 128], maskb[:], ALU.mult, [('Pb', pbi), 'maskb'], [('Pb', pbi)])
                                mm(pa[m][:, lo:512], vh[slot][:, j, :], P[:, lo:512], j == 0, j == nj - 1, [('vh', slot), ('Pb', pbi)], [('pa', m)])
                                mm(pa[2 + m][:, lo:512], onesb[:], P[:, lo:512], j == 0, j == nj - 1, ['onesb', ('Pb', pbi)], [('pa', 2 + m)])
                        finalize(hd, qc)

            def bias_B(P, ps, pkey, Pkey, hd, j, qc, lo, scale):
                if j < 8:
                    act(P[:, lo:512], ps[:, lo:512], AF.Exp, [pkey, 'cfgs'], [Pkey], bias=prevb, scale=scale)
                else:
                    act(P[:, lo:512], ps[:, lo:512], AF.Exp, [pkey], [Pkey], scale=scale)

            def fin_B(hd, qc):
                S.op('dve', lambda e: e.reciprocal(out=r1, in_=pa[2][:]), reads=[('pa', 2)], writes=['r1'])
                S.op('dve', lambda e: e.reciprocal(out=r2, in_=pa[3][:]), reads=[('pa', 3)], writes=['r2'])
                tt('dve', oa, pa[0][:], r1, ALU.mult, [('pa', 0), 'r1'], ['oa'])
                tt('dve', ob, pa[1][:], r2, ALU.mult, [('pa', 1), 'r2'], ['ob'])
                stt('dve', oa, ob, neglam, oa, ALU.mult, ALU.add, ['ob', 'oa', 'lam'], ['oa'])
                act(sqb, oa, AF.Square, ['oa'], ['sqb'])
                mm(pa[2][:], onesb[:], sqb, True, True, ['onesb', 'sqb'], [('pa', 2)])
                rstd_from(r1, pa[2][:], 1.0 / 128, [('pa', 2)], ['r1'])
                tt('dve', oa, oa, r1, ALU.mult, ['oa', 'r1'], ['oa'])
                act(oT[:, 4 + hd, qc * 512:(qc + 1) * 512], oa, AF.Copy, ['oa', 'lam'], [('oT', 1)], scale=subg)

            attention(2, QB, 0, 1024, fin_B, bias_B, 64 ** -0.5)
            if l == dbg_layer:
                dump('oB', oT[:, 4, :], [('oT', 1)])

            if stop < 5:
                break
            barrier()
            dma('sp', nl_all[:, 0:8, :], c2o[0:4, :].rearrange("r (t h) -> (r t) h", h=4).rearrange("(i p) h -> p i h", p=128), ['c2o'], ['nl_all'], 'nla')
            dma('sp', nl_all[:, 8:16, :], c2i[0:4, :].rearrange("r (t h) -> (r t) h", h=4).rearrange("(i p) h -> p i h", p=128), ['c2i'], ['nl_all'], 'nla')
            nlf = Xf[:, 7168:7232]
            p = nxt('pa', 4)
            mm(pa[p][:, 0:64], ones_f, nlf, True, True, ['cst', 'nl_all'], [('pa', p)])
            cp('dve', tot, pa[p][:, 0:64].rearrange("p (j h) -> p j h", j=16), [('pa', p)], ['tot'])
            S.op('dve', lambda e: e.memset(pref[:, 0, :], 0.0), reads=[], writes=['pref'])
            for j in range(1, 16):
                tt('dve', pref[:, j, :], pref[:, j - 1, :], tot[:, j - 1, :], ALU.add, ['pref', 'tot'], ['pref'])
            p = nxt('pa', 4)
            mm(pa[p][:, 0:64], triu_f, nlf, True, True, ['cst', 'nl_all'], [('pa', p)])
            tt('dve', cum, pa[p][:, 0:64].rearrange("p (j h) -> p j h", j=16), pref, ALU.add, [('pa', p), 'pref'], ['cum'])
            p = nxt('pa', 4)
            mm(pa[p][:, 0:64], half_f, nlf, True, True, ['cst', 'nl_all'], [('pa', p)])
            stt('dve', crefn, pa[p][:, 32:64].rearrange("p (q h) -> p q h", q=8), -1.0, pref[:, 8:16, :], ALU.mult, ALU.subtract,
                [('pa', p), 'pref'], ['crefn'])
            for hd in range(4):
                for j in range(16):
                    if j < 8:
                        ts('dve', btab[:, hd, j, :], crefn[:, :, hd], cum[:, j, hd:hd + 1], prevb, ALU.add, ALU.add, ['crefn', 'cum', 'cfgs'], ['btab'])
                    else:
                        ts('dve', btab[:, hd, j, :], crefn[:, :, hd], cum[:, j, hd:hd + 1], None, ALU.add, None, ['crefn', 'cum'], ['btab'])

            def bias_C(P, ps, pkey, Pkey, hd, j, qc, lo, scale):
                for qt in range(lo // 128, 4):
                    act(P[:, qt * 128:(qt + 1) * 128], ps[:, qt * 128:(qt + 1) * 128], AF.Exp, [pkey, 'btab'], [Pkey],
                        bias=btab[:, hd, j, 4 * qc + qt:4 * qc + qt + 1], scale=scale)

            def fin_C(hd, qc):
                S.op('dve', lambda e: e.reciprocal(out=r1, in_=pa[2][:]), reads=[('pa', 2)], writes=['r1'])
                tt('dve', oT[:, 8 + hd, qc * 512:(qc + 1) * 512], pa[0][:], r1, ALU.mult, [('pa', 0), 'r1'], [('oT', 2)])

            attention(1, QC, 512, 1536, fin_C, bias_C, 128 ** -0.5)
            if l == dbg_layer:
                dump('oC', oT[:, 8, :], [('oT', 2)])

            if stop < 6:
                break
            barrier()
            acc = Xf[:, 0:4096].rearrange("p (c t) -> p c t", c=4)
            for dg in range(4):
                for n in range(4):
                    b = load_w(w_in_l[:, GATE + n * D + dg * 512:GATE + n * D + (dg + 1) * 512], 512)
                    wb_i = nxt('wbr', 2)
                    dma('pool', wbr[wb_i][:], w_branch[lw, n][:, dg * 512:(dg + 1) * 512].rearrange("(k p) c -> p k c", p=128), [], [('wbr', wb_i)],
                        ('wbr', wb_i), nobarrier=True)
                    for dc in range(4):
                        for hf in range(2):
                            pg = nxt('pa', 6)
                            for k in range(KC):
                                mm(pa[pg][:], wbuf[b][:, k, dc * 128:(dc + 1) * 128], xnT[:, k, hf * 512:(hf + 1) * 512], k == 0, k == KC - 1,
                                   [('w', b)], [('pa', pg)])
                            pb = nxt('pa', 6)
                            for cc in range(4):
                                mm(pa[pb][:], wbr[wb_i][:, cc, dc * 128:(dc + 1) * 128], oT[:, n * 4 + cc, hf * 512:(hf + 1) * 512], cc == 0, cc == 3,
                                   [('wbr', wb_i), ('oT', n)], [('pa', pb)])
                            s = nxt('stg', 2)
                            act(stg[s][:], pa[pg][:], AF.Sigmoid, [('pa', pg)], [('stg', s)])
                            av = acc[:, dc, hf * 512:(hf + 1) * 512]
                            akey = ('acc', dc, hf)
                            if n == 0:
                                tt('dve', av, pa[pb][:], stg[s][:], ALU.mult, [('pa', pb), ('stg', s)], [akey])
                            else:
                                tt('dve', stg[s][:], pa[pb][:], stg[s][:], ALU.mult, [('pa', pb), ('stg', s)], [('stg', s)])
                                if n < 3:
                                    tt('dve', av, av, stg[s][:], ALU.add, [akey, ('stg', s)], [akey])
                                else:
                                    tt('dve', merged[:, dg * 4 + dc, hf * 512:(hf + 1) * 512], av, stg[s][:], ALU.add, [akey, ('stg', s)],
                                       [('dstT', i) for i in range(hf * 4, hf * 4 + 4)])
            if l == dbg_layer:
                dump('merged', merged[:, 0, :], [('dstT', i) for i in range(NT)])

            if stop < 7:
                break
            barrier()
            dma('sp', gbc[:], norm_mix_post[lw].partition_broadcast(128), [], ['gbc'], 'gbc')

            def wo_evac(cg):
                def f(i, ps, pkey):
                    act(yv[:, i, cg * 512:(cg + 1) * 512], ps, AF.Copy, [pkey], [('y', i)])
                    sbi = nxt('stgb', 2)
                    act(stgb[sbi][:], ps, AF.Square, [pkey], [('stgb', sbi), ('ssq', i)], accum=ssq[:, i, cg:cg + 1])
                return f

            for cg in range(4):
                proj_tok(0, 512, wo_evac(cg), wsrc=w_out[lw][:, cg * 512:(cg + 1) * 512], lhs=merged)

            def residual(dst_dram, src_dram):
                for i in range(NT):
                    hb = nxt('ht', 2)
                    ht = htb[hb]
                    dma('sp', ht, src_dram[i * 128:(i + 1) * 128, :], [('hbuf', i)], [('ht', hb)], ('ht', hb))
                    S.op('dve', lambda e, i=i: e.reduce_sum(out=stat[:, 16:17], in_=ssq[:, i, :], axis=AX.X), reads=[('ssq', i)], writes=['stat'])
                    rstd_from(stat[:, 17:18], stat[:, 16:17], 1.0 / D, ['stat'], ['stat'])
                    stt('dve', yv[:, i, :], yv[:, i, :], stat[:, 17:18], gbc[:], ALU.mult, ALU.mult, [('y', i), 'stat', 'gbc'], [('y', i)])
                    tt('dve', ht, ht, yv[:, i, :], ALU.add, [('ht', hb), ('y', i)], [('ht', hb)])
                    dma('sp', dst_dram[i * 128:(i + 1) * 128, :], ht, [('ht', hb)], [('hbuf', i)], ('hto', hb))

            residual(hbuf, hsrc)
            if l == dbg_layer:
                barrier()
                dump('h1', hbuf[0:128, :], [('hbuf', 0)])

            if stop < 8:
                break
            barrier()
            norm_transpose(hbuf, norm_ffn_pre[lw], hnT)
            barrier()
            ffT = [X[:, 0:8192].rearrange("p (c t) -> p c t", c=8), X[:, 8192:16384].rearrange("p (c t) -> p c t", c=8)]
            for fb in range(8):
                fs = fb % 2
                for sub in range(2):
                    def up_evac(c, hf, ps, pkey, sub=sub, fs=fs):
                        s = nxt('stg', 2)
                        act(stg[s][:], ps, AF.Relu, [pkey], [('stg', s)])
                        tt('dve', ffT[fs][:, sub * 4 + c, hf * 512:(hf + 1) * 512], stg[s][:], stg[s][:], ALU.mult, [('stg', s)],
                           [('ffT', fs, i) for i in range(hf * 4, hf * 4 + 4)])
                    proj_feat(0, up_evac, wsrc=w_ffn_up[lw][:, (fb * 2 + sub) * 512:(fb * 2 + sub + 1) * 512], rhsT=hnT)
                for cg in range(4):
                    b = load_w(w_ffn_down[lw][fb * 1024:(fb + 1) * 1024, cg * 512:(cg + 1) * 512], 512)
                    for i in range(NT):
                        p = nxt('pa', 6)
                        for fc in range(8):
                            mm(pa[p][:], ffT[fs][:, fc, i * 128:(i + 1) * 128], wbuf[b][:, fc, :], fc == 0, fc == 7, [('ffT', fs, i), ('w', b)], [('pa', p)])
                        yy = yv[:, i, cg * 512:(cg + 1) * 512]
                        if fb == 0:
                            act(yy, pa[p][:], AF.Copy, [('pa', p)], [('y', i)])
                        else:
                            tt('dve', yy, pa[p][:], yy, ALU.add, [('pa', p), ('y', i)], [('y', i)])
            barrier()
            dma('sp', gbc[:], norm_ffn_post[lw].partition_broadcast(128), [], ['gbc'], 'gbc')
            for i in range(NT):
                act(xb, yv[:, i, 0:D], AF.Square, [('y', i)], ['xb', ('ssq', i)], accum=ssq[:, i, 0:1])
                S.op('dve', lambda e, i=i: e.memset(ssq[:, i, 1:4], 0.0), reads=[], writes=[('ssq', i)])
            residual(out if l == depth - 1 else hbuf, hbuf)
            if l == dbg_layer and l < depth - 1:
                barrier()
                dump('hend', hbuf[0:128, :], [('hbuf', 0)])
                dump('yend', yv[:, 0, :], [('y', 0)])

        S.op('sp', None, reads=[('hbuf', i) for i in range(NT)], writes=[('hbuf', i) for i in range(NT)] + ['PHASE'])
        S.emit(block, st)
        build.stats = (len(S.ops), dict(S.sem_counts))
    return nc


def host_inputs(inputs, depth=2):
    f32 = np.float32
    x = np.ascontiguousarray(np.asarray(inputs["x"], dtype=f32))
    positions = np.asarray(inputs["positions"]).astype(np.int32)
    consts = np.zeros((128, 5, 128), f32)
    r = np.arange(128)
    consts[:, 0, :] = np.eye(128, dtype=f32)
    consts[:, 1, :] = (r[None, :] >= r[:, None]).astype(f32)
    consts[:, 2, :] = (r[None, :] <= r[:, None]).astype(f32)
    consts[:, 3, :] = 1.0
    consts[:, 4, :] = (r[:, None] <= 63).astype(f32)
    inv = (1.0 / (np.float32(500000.0) ** (np.arange(0, 16, 2, dtype=f32) / np.float32(16)))).astype(f32)
    pvec = np.zeros((2, 128, 5), f32)
    ps = np.asarray(inputs["pool_scale"], dtype=f32).reshape(2, 4, 128)
    pvec[:, :, 0:4] = ps.transpose(0, 2, 1)
    pvec[:, :, 4] = np.asarray(inputs["da_subln_g"], dtype=f32)
    shared = {k: np.ascontiguousarray(np.asarray(inputs[k], dtype=f32)) for k in
              ["norm_mix_pre", "norm_mix_post", "norm_ffn_pre", "norm_ffn_post", "w_in", "gm_ln_g", "gm_ln_b", "gm_w_s", "gm_b_s",
               "da_lambda", "fa_b_f", "pool_w", "w_branch", "w_out", "w_ffn_up", "w_ffn_down"]}
    shared["pvec"] = pvec
    shared["consts"] = consts
    in_maps = []
    for c in range(8):
        b, half = c // 2, c % 2
        cfg = np.zeros((128, 74), f32)
        cfg[:, 0] = 0.0 if half == 1 else NEG
        cfg[:, 1] = 1.0 if half == 1 else 0.0
        for g in range(4):
            w = 2 << g
            t = np.arange(16)
            cnt = np.minimum(t + 1, w) if half == 0 else np.full(16, w)
            cfg[:, 2 + g * 16:2 + (g + 1) * 16] = (1.0 / cnt.astype(f32))[None, :]
        cfg[:, 66:74] = inv[None, :]
        m = dict(shared)
        m["x"] = np.ascontiguousarray(x[b, half * 1024:(half + 1) * 1024, :])
        m["pos"] = np.ascontiguousarray(positions[b, half * 1024:(half + 1) * 1024].reshape(NT, 128).T)
        m["cfg"] = cfg
        in_maps.append(m)
    return in_maps


_NC_CACHE = {}


def _get(mode, lb):
    key = (mode, lb)
    if key not in _NC_CACHE:
        _NC_CACHE[key] = build(1, mode=mode, layer_base=lb)
    return _NC_CACHE[key]


def kernel(**inputs):
    in_maps = host_inputs(inputs)
    for l in range(2):
        ncA = _get('A', l)
        resA = run_bass_kernel_spmd(ncA, in_maps, core_ids=list(range(8)))
        mapsB = []
        for c in range(8):
            e = c - c % 2
            m = dict(in_maps[c])
            m["prev1"] = np.ascontiguousarray(resA.results[e]["c1i_out"])
            m["prev2"] = np.ascontiguousarray(np.concatenate([resA.results[e]["c2i_out"], resA.results[e + 1]["c2i_out"]], axis=0))
            mapsB.append(m)
        ncB = _get('B', l)
        resB = run_bass_kernel_spmd(ncB, mapsB, core_ids=list(range(8)))
        for c in range(8):
            in_maps[c] = dict(in_maps[c])
            in_maps[c]["x"] = np.ascontiguousarray(resB.results[c]["out"])
    outp = np.zeros((4, 2048, 2048), np.float32)
    for c in range(8):
        b, half = c // 2, c % 2
        outp[b, half * 1024:(half + 1) * 1024, :] = in_maps[c]["x"]
    return outp
```

```python
import contextlib
import math
import numpy as np
import concourse.bass as bass
import concourse.mybir as mybir
from concourse.bass_utils import run_bass_kernel_spmd

F32 = mybir.dt.float32
BF16 = mybir.dt.bfloat16
I32 = mybir.dt.int32
AF = mybir.ActivationFunctionType
ALU = mybir.AluOpType
AX = mybir.AxisListType

D = 2048
T = 1024
NT = 8
KC = 16
IN_COLS = 12804
A_U, A_V, B_Q, B_K, B_V, C_Q, C_K, C_V, C_F, D_H, GATE = 0, 512, 1024, 1536, 2048, 2560, 3072, 3584, 4096, 4100, 4612
EPS = 1e-6
NEG = -30000.0


class Sched:
    ENGS = ['pe', 'act', 'dve', 'pool', 'sp']

    def __init__(self, nc, same_engine_sync=True):
        self.nc = nc
        self.ops = []
        self.last_w = {}
        self.readers = {}
        self.same_engine_sync = same_engine_sync

    def op(self, eng, fn, reads=(), writes=(), dma=None, inc=16, nobarrier=False):
        idx = len(self.ops)
        reads = list(reads)
        writes = list(writes)
        if not nobarrier:
            reads.append('PHASE')
        deps = set()
        for k in reads + writes:
            w = self.last_w.get(k)
            if w is not None:
                deps.add(w)
        for k in writes:
            for r in self.readers.get(k, {}).values():
                deps.add(r)
        self.ops.append(dict(eng=eng, fn=fn, deps=deps, dma=dma, sig=None, inc=inc))
        for k in writes:
            self.last_w[k] = idx
            self.readers[k] = {}
        rk = eng if dma is None else ('dma', idx)
        for k in reads:
            self.readers.setdefault(k, {})[rk] = idx
        return idx

    def _skip(self, o, d):
        if d['dma'] is not None or o['dma'] is not None:
            return False
        if d['eng'] == o['eng']:
            if d['eng'] == 'pe':
                return True
            return not self.same_engine_sync
        return False

    def emit(self, block, stack):
        nc = self.nc
        ops = self.ops
        needed = [False] * len(ops)
        for o in ops:
            for d in o['deps']:
                if not self._skip(o, ops[d]):
                    needed[d] = True
        cnt = {}
        for i, o in enumerate(ops):
            if o['fn'] is None:
                continue
            if o['dma'] is not None:
                key = ('dma', o['dma'])
                cnt[key] = cnt.get(key, 0) + o['inc']
                o['sig'] = (key, cnt[key])
            elif needed[i]:
                key = ('eng', o['eng'])
                cnt[key] = cnt.get(key, 0) + 1
                o['sig'] = (key, cnt[key])
        sems = {}
        for n, key in enumerate(cnt):
            sems[key] = stack.enter_context(nc.semaphore("s%d" % n))
        self.sem_counts = cnt
        per_eng = {e: [i for i, o in enumerate(ops) if o['eng'] == e] for e in self.ENGS}

        def run(engname, e):
            waited = {}
            for i in per_eng[engname]:
                o = ops[i]
                need = {}
                for d in o['deps']:
                    dd = ops[d]
                    if dd['sig'] is None or self._skip(o, dd):
                        continue
                    key, val = dd['sig']
                    if val > need.get(key, 0):
                        need[key] = val
                for key, val in need.items():
                    if waited.get(key, 0) >= val:
                        continue
                    e.wait_ge(sems[key], val)
                    waited[key] = val
                if o['fn'] is not None:
                    ins = o['fn'](e)
                    if o['sig'] is not None:
                        key, val = o['sig']
                        if o['dma'] is not None and o['inc'] == 1:
                            ins.then_inc(sems[key])
                        else:
                            ins.then_inc(sems[key], o['inc'] if o['dma'] is not None else 1)

        @block.tensor
        def _(e):
            run('pe', e)

        @block.scalar
        def _(e):
            run('act', e)

        @block.vector
        def _(e):
            run('dve', e)

        @block.gpsimd
        def _(e):
            run('pool', e)

        @block.sync
        def _(e):
            run('sp', e)


def build(depth=2, dbg=None, stop=99, nocc=False, dbg_layer=0, mode='fused', layer_base=0):
    nc = bass.Bass("TRN2", target_bir_lowering=False)

    def din(name, shape, dt=F32):
        return nc.dram_tensor(name, shape, dt, kind="ExternalInput").ap()

    x = din("x", [T, D])
    pos = din("pos", [128, NT], I32)
    cfg = din("cfg", [128, 74])
    consts = din("consts", [128, 5, 128])
    norm_mix_pre = din("norm_mix_pre", [2, D])
    norm_mix_post = din("norm_mix_post", [2, D])
    norm_ffn_pre = din("norm_ffn_pre", [2, D])
    norm_ffn_post = din("norm_ffn_post", [2, D])
    w_in = din("w_in", [2, D, IN_COLS])
    gm_ln_g = din("gm_ln_g", [2, 512])
    gm_ln_b = din("gm_ln_b", [2, 512])
    gm_w_s = din("gm_w_s", [2, 4, 128, 128])
    gm_b_s = din("gm_b_s", [2, 4, 128])
    da_lambda = din("da_lambda", [2, 4, 64])
    fa_b_f = din("fa_b_f", [2, 4])
    pool_w = din("pool_w", [2, 4, 128, 128])
    pvec = din("pvec", [2, 128, 5])
    w_branch = din("w_branch", [2, 4, 512, D])
    w_out = din("w_out", [2, D, D])
    w_ffn_up = din("w_ffn_up", [2, D, 4 * D])
    w_ffn_down = din("w_ffn_down", [2, 4 * D, D])
    if mode == 'A':
        stop = 1
        nocc = True
    if mode in ('B', 'B2'):
        nocc = True
    if mode == 'B2':
        depth = 2
    out = None if mode == 'A' else nc.dram_tensor("out", [T, D], F32, kind="ExternalOutput").ap()
    hbuf = nc.dram_tensor("hbuf", [T, D], F32).ap()
    if mode == 'A':
        cc1i = [nc.dram_tensor("c1i_out", [2048, 1024], BF16, kind="ExternalOutput")] * depth
        cc2i = [nc.dram_tensor("c2i_out", [12, 1024], F32, kind="ExternalOutput")] * depth
    elif mode == 'B2':
        cc1i = [nc.dram_tensor("cc1i", [2048, 1024], BF16), nc.dram_tensor("c1i_out", [2048, 1024], BF16, kind="ExternalOutput")]
        cc2i = [nc.dram_tensor("cc2i", [12, 1024], F32), nc.dram_tensor("c2i_out", [12, 1024], F32, kind="ExternalOutput")]
    else:
        cc1i = [nc.dram_tensor("cc1i", [2048, 1024], BF16)] * depth
        cc2i = [nc.dram_tensor("cc2i", [12, 1024], F32)] * depth
    if mode in ('B', 'B2'):
        prev1 = nc.dram_tensor("prev1", [2048, 1024], BF16, kind="ExternalInput")
        prev2 = nc.dram_tensor("prev2", [24, 1024], F32, kind="ExternalInput")
        cc1o = None
        cc2o = [prev2] * depth
    else:
        cc1o = [[nc.dram_tensor("cc1o_%d" % r, [512, 1024], BF16) for r in range(8)]] * depth
        cc2o = [nc.dram_tensor("cc2o", [24, 1024], F32)] * depth
    dbg_out = {}
    if dbg:
        for name, shape in dbg.items():
            dbg_out[name] = nc.dram_tensor("dbg_" + name, shape, F32, kind="ExternalOutput").ap()
    RG = [[0, 1], [2, 3], [4, 5], [6, 7]]

    with contextlib.ExitStack() as st:
        def sb(name, shape, dt):
            return st.enter_context(nc.sbuf_tensor(name, shape, dt))

        A1 = sb("A1", [128, 32768], BF16)
        A2 = sb("A2", [128, 16384], BF16)
        X = sb("X", [128, 16384], BF16)
        wbuf = [sb("wbuf%d" % i, [128, KC, 512], BF16) for i in range(2)]
        wbr = [sb("wbr%d" % i, [128, 4, 512], BF16) for i in range(2)]
        gbc = sb("gbc", [128, D], F32)
        QB = sb("QB", [128, 4, T], BF16)
        QC = sb("QC", [128, 4, T], BF16)
        cst = sb("cst", [128, 5, 128], F32)
        identb = sb("identb", [128, 128], BF16)
        maskb = sb("maskb", [128, 128], BF16)
        onesb = sb("onesb", [128, 128], BF16)
        cfgs = sb("cfgs", [128, 74], F32)
        posi = sb("posi", [128, NT], I32)
        posf = sb("posf", [128, NT], F32)
        ang = sb("ang", [128, NT, 8], F32)
        cosT = sb("cosT", [128, NT, 8], F32)
        sinT = sb("sinT", [128, NT, 8], F32)
        stat = sb("stat", [128, 64], F32)
        ssq = sb("ssq", [128, NT, 4], F32)
        stg = [sb("stg%d" % i, [128, 512], F32) for i in range(2)]
        stgb = [sb("stgb%d" % i, [128, 512], BF16) for i in range(2)]
        rt = sb("rt", [128, 4, 8, 8], F32)
        pv = sb("pv", [128, 5], F32)
        lam = sb("lam", [128, 8], F32)
        bfb = sb("bfb", [128, 4], F32)
        wf = sb("wf", [128, KC, 4], BF16)
        scr = sb("scr", [128, 8], F32)
        epsc = sb("epsc", [128, 1], F32)
        kint = sb("kint", [128, NT, 8], I32)
        pa = [st.enter_context(nc.psum_tensor("pa%d" % i, [128, 512], F32)) for i in range(6)]
        pt = [st.enter_context(nc.psum_tensor("pt%d" % i, [128, 1024], BF16)) for i in range(2)]
        block = st.enter_context(nc.Block())
        S = Sched(nc)

        xnT = A1[:, 0:16384].rearrange("p (k t) -> p k t", k=KC)
        oT = A1[:, 16384:32768].rearrange("p (c t) -> p c t", c=16)
        yv = A1[:, :].bitcast(F32).rearrange("p (i c) -> p i c", i=NT)
        merged = A2[:, :].rearrange("p (k t) -> p k t", k=KC)
        hnT = merged
        Xf = X[:, :].bitcast(F32)
        A2f = A2[:, :].bitcast(F32)
        htb = [Xf[:, 0:2048], Xf[:, 2048:4096]]
        xb = X[:, 8192:10240]
        ident_f = cst[:, 0, :]
        triu_f = cst[:, 1, :]
        tril_f = cst[:, 2, :]
        ones_f = cst[:, 3, :]
        half_f = cst[:, 4, :]
        prevb = cfgs[:, 0:1]
        flag = cfgs[:, 1:2]
        rcnt = cfgs[:, 2:66].rearrange("p (g t) -> p g t", g=4)
        invf = cfgs[:, 66:74]

        cnts = {'pa': 0, 'w': 0, 'stg': 0, 'stgb': 0, 'ht': 0, 'wbr': 0}

        def nxt(kind, n):
            v = cnts[kind] % n
            cnts[kind] += 1
            return v

        def mm(o, lhsT, rhs, start, stop, reads, writes):
            S.op('pe', lambda e: e.matmul(o, lhsT=lhsT, rhs=rhs, start=start, stop=stop), reads=reads, writes=writes)

        def tp(o, in_, ident, reads, writes):
            S.op('pe', lambda e: e.transpose(out=o, in_=in_, identity=ident), reads=reads, writes=writes)

        def act(o, in_, func, reads, writes, bias=None, scale=None, accum=None):
            kw = {}
            if bias is not None:
                kw['bias'] = bias
            if scale is not None:
                kw['scale'] = scale
            if accum is not None:
                kw['accum_out'] = accum
            S.op('act', lambda e: e.activation(out=o, in_=in_, func=func, **kw), reads=reads, writes=writes)

        def tt(eng, o, a, b, op, reads, writes):
            S.op(eng, lambda e: e.tensor_tensor(out=o, in0=a, in1=b, op=op), reads=reads, writes=writes)

        def ts(eng, o, a, s1, s2, op0, op1, reads, writes):
            if s2 is None:
                S.op(eng, lambda e: e.tensor_scalar(out=o, in0=a, scalar1=s1, scalar2=None, op0=op0), reads=reads, writes=writes)
            else:
                S.op(eng, lambda e: e.tensor_scalar(out=o, in0=a, scalar1=s1, scalar2=s2, op0=op0, op1=op1), reads=reads, writes=writes)

        def stt(eng, o, a, s, b, op0, op1, reads, writes):
            S.op(eng, lambda e: e.scalar_tensor_tensor(out=o, in0=a, scalar=s, in1=b, op0=op0, op1=op1), reads=reads, writes=writes)

        def cp(eng, o, a, reads, writes):
            S.op(eng, lambda e: e.tensor_copy(out=o, in_=a), reads=reads, writes=writes)

        def dma(q, o, in_, reads, writes, key, nobarrier=False):
            S.op(q, lambda e: e.dma_start(out=o, in_=in_), reads=reads, writes=writes, dma=key, nobarrier=nobarrier)

        def barrier():
            S.op('dve', lambda e: e.memset(scr[:, 0:1], 0.0), reads=[], writes=['PHASE', 'scr'])

        def rstd_from(o, a, scale, reads, writes):
            act(o, a, AF.Sqrt, reads, writes, bias=epsc[:, 0:1], scale=scale)
            S.op('dve', lambda e: e.reciprocal(out=o, in_=o), reads=writes, writes=writes)

        def load_w(src, ncols=512):
            b = nxt('w', 2)
            kk = src.shape[0] // 128
            dma('pool', wbuf[b][:, 0:kk, 0:ncols], src.rearrange("(k p) c -> p k c", p=128), [], [('w', b)], ('w', b), nobarrier=True)
            return b

        def dump(name, src_ap, reads):
            if name in dbg_out:
                dma('pool', dbg_out[name], src_ap, reads, [], ('dbg', name))

        dma('sp', cst[:], consts, [], ['cst'], 'cst')
        dma('sp', cfgs[:], cfg, [], ['cfgs'], 'cfgs')
        dma('sp', posi[:], pos, [], ['posi'], 'posi')
        cp('dve', identb[:], ident_f, ['cst'], ['identb'])
        cp('dve', maskb[:], triu_f, ['cst'], ['maskb'])
        cp('dve', onesb[:], ones_f, ['cst'], ['onesb'])
        cp('dve', posf[:], posi[:], ['posi'], ['posf'])
        for i in range(NT):
            ts('dve', ang[:, i, :], invf, posf[:, i:i + 1], None, ALU.mult, None, ['posf', 'cfgs'], ['ang'])
        S.op('dve', lambda e: e.memset(epsc[:], EPS), reads=[], writes=['epsc'])

        def sin_of(dst, shift):
            a2 = rt[:, 0:2].rearrange("p a i j -> p (a i) j")[:, 0:8, :]
            kf = rt[:, 2:4].rearrange("p a i j -> p (a i) j")[:, 0:8, :]
            ts('dve', a2, ang[:], shift, None, ALU.add, None, ['ang'], ['rt'])
            ts('dve', kf, a2, 1.0 / (2 * math.pi), None, ALU.mult, None, ['rt'], ['rt'])
            cp('dve', kint[:], kf, ['rt'], ['kint'])
            cp('dve', kf, kint[:], ['kint'], ['rt'])
            stt('dve', a2, kf, -6.28125, a2, ALU.mult, ALU.add, ['rt'], ['rt'])
            stt('dve', a2, kf, -(2 * math.pi - 6.28125), a2, ALU.mult, ALU.add, ['rt'], ['rt'])
            ts('dve', kf, a2, math.pi, -2 * math.pi, ALU.is_gt, ALU.mult, ['rt'], ['rt'])
            tt('dve', a2, a2, kf, ALU.add, ['rt'], ['rt'])
            ts('dve', kf, a2, -math.pi, 2 * math.pi, ALU.is_lt, ALU.mult, ['rt'], ['rt'])
            tt('dve', a2, a2, kf, ALU.add, ['rt'], ['rt'])
            ts('dve', a2, a2, math.pi, -math.pi, ALU.min, ALU.max, ['rt'], ['rt'])
            act(dst, a2, AF.Sin, ['rt'], [dst.name if False else 'trig'])

        sin_of(sinT[:], 0.0)
        sin_of(cosT[:], 0.5 * math.pi)

        def norm_transpose(src, gain, dstT):
            dma('sp', gbc[:], gain.partition_broadcast(128), [], ['gbc'], 'gbc')
            for i in range(NT):
                hb = nxt('ht', 2)
                ht = htb[hb]
                dma('sp', ht, src[i * 128:(i + 1) * 128, :], [('hbuf', i)], [('ht', hb)], ('ht', hb))
                act(xb, ht, AF.Square, [('ht', hb)], ['xb', 'stat'], accum=stat[:, 0:1])
                rstd_from(stat[:, 1:2], stat[:, 0:1], 1.0 / D, ['stat'], ['stat'])
                stt('dve', xb, ht, stat[:, 1:2], gbc[:], ALU.mult, ALU.mult, [('ht', hb), 'stat', 'gbc'], ['xb'])
                for hf in range(2):
                    for k in range(8):
                        kk = hf * 8 + k
                        tp(pt[hf][:, k * 128:(k + 1) * 128], xb[:, kk * 128:(kk + 1) * 128], identb[:], ['xb', 'identb'], [('pt', hf)])
                    src_v = pt[hf][:, :].rearrange("p (k t) -> p k t", k=8)
                    dst_v = dstT[:, hf * 8:(hf + 1) * 8, i * 128:(i + 1) * 128]
                    if hf == 0:
                        act(dst_v, src_v, AF.Copy, [('pt', hf)], [('dstT', i)])
                    else:
                        cp('dve', dst_v, src_v, [('pt', hf)], [('dstT', i)])

        def proj_tok(col0, ncols, evac, wsrc=None, lhs=None, lhs_key='dstT'):
            src = w_in_l[:, col0:col0 + ncols] if wsrc is None else wsrc
            b = load_w(src, ncols)
            L = xnT if lhs is None else lhs
            for i in range(NT):
                p = nxt('pa', 6)
                for k in range(KC):
                    mm(pa[p][:, 0:ncols], L[:, k, i * 128:(i + 1) * 128], wbuf[b][:, k, 0:ncols], k == 0, k == KC - 1,
                       [(lhs_key, i), ('w', b)], [('pa', p)])
                evac(i, pa[p][:, 0:ncols], ('pa', p))

        def proj_feat(col0, evac, wsrc=None, rhsT=None, nchunk=4):
            src = w_in_l[:, col0:col0 + 128 * nchunk] if wsrc is None else wsrc
            b = load_w(src, 128 * nchunk)
            R = xnT if rhsT is None else rhsT
            for c in range(nchunk):
                for hf in range(2):
                    p = nxt('pa', 6)
                    for k in range(KC):
                        mm(pa[p][:], wbuf[b][:, k, c * 128:(c + 1) * 128], R[:, k, hf * 512:(hf + 1) * 512], k == 0, k == KC - 1,
                           [('dstT', i) for i in range(hf * 4, hf * 4 + 4)] + [('w', b)], [('pa', p)])
                    evac(c, hf, pa[p][:], ('pa', p))

        for l in range(depth):
            lw = layer_base + l
            w_in_l = w_in[lw]
            lam_init = 0.8 - 0.6 * math.exp(-0.3 * lw)
            hsrc = x if l == 0 else hbuf
            c1i, c2i, c2o = cc1i[l].ap(), cc2i[l].ap(), cc2o[l].ap()
            c1oc = [prev1.ap()[r * 256:(r + 1) * 256, :] for r in range(8)] if mode in ('B', 'B2') else [t.ap() for t in cc1o[l]]
            barrier()
            dma('sp', pv[:], pvec[lw], [], ['pv'], 'pv')
            lpb = stg[1]
            dma('sp', lpb[:, 0:256], da_lambda[lw].rearrange("a d -> (a d)").partition_broadcast(128), [], [('stg', 1)], 'lpb')
            dma('sp', bfb[:], fa_b_f[lw].partition_broadcast(128), [], ['bfb'], 'bfb')
            lv = lpb[:, 0:256].rearrange("p (a b d) -> p a b d", a=2, b=2)
            lprod = stg[0][:, 0:128].rearrange("p (a d) -> p a d", a=2)
            tt('dve', lprod, lv[:, :, 0, :], lv[:, :, 1, :], ALU.mult, [('stg', 1)], [('stg', 0)])
            S.op('dve', lambda e: e.reduce_sum(out=lam[:, 0:2], in_=lprod, axis=AX.X),
                 reads=[('stg', 0)], writes=['lam'])
            act(lam[:, 2:4], lam[:, 0:2], AF.Exp, ['lam'], ['lam'])
            tt('dve', lam[:, 4:5], lam[:, 3:4], lam[:, 2:3], ALU.subtract, ['lam'], ['lam'])
            ts('dve', lam[:, 4:5], lam[:, 4:5], -lam_init, None, ALU.add, None, ['lam'], ['lam'])
            ts('dve', lam[:, 5:6], pv[:, 4:5], 1.0 - lam_init, None, ALU.mult, None, ['pv'], ['lam'])
            neglam = lam[:, 4:5]
            subg = lam[:, 5:6]

            norm_transpose(hsrc, norm_mix_pre[lw], xnT)
            barrier()
            if l == dbg_layer:
                dump('xnT', xnT[:, 0, :], [('dstT', i) for i in range(NT)])

            kTst = X[:, 0:4096].rearrange("p (a t) -> p a t", a=4)
            vst = X[:, 4096:8192].rearrange("p (i c) -> p i c", i=NT)
            hD = A2f[:, 0:4160].rearrange("p (g t) -> p g t", g=4)
            nlst = A2f[:, 4160:4192].rearrange("p (i h) -> p i h", i=NT)

            def qk_evac(which):
                def f(i, ps, pkey):
                    s = nxt('stg', 2)
                    g = stg[s]
                    act(g[:], ps, AF.Copy, [pkey], [('stg', s)])
                    gv = g[:, :].rearrange("p (m d) -> p m d", m=8)
                    x1 = gv[:, :, 0:8]
                    x2 = gv[:, :, 8:16]
                    cb = cosT[:, i:i + 1, :].to_broadcast([128, 8, 8])
                    sn = sinT[:, i:i + 1, :].to_broadcast([128, 8, 8])
                    tt('dve', rt[:, 0], x1, cb, ALU.mult, [('stg', s), 'trig'], ['rt'])
                    tt('dve', rt[:, 1], x2, sn, ALU.mult, [('stg', s), 'trig'], ['rt'])
                    tt('dve', rt[:, 2], x1, sn, ALU.mult, [('stg', s), 'trig'], ['rt'])
                    tt('dve', rt[:, 3], x2, cb, ALU.mult, [('stg', s), 'trig'], ['rt'])
                    tt('dve', x1, rt[:, 0], rt[:, 1], ALU.subtract, ['rt'], [('stg', s)])
                    tt('dve', x2, rt[:, 2], rt[:, 3], ALU.add, ['rt'], [('stg', s)])
                    sbi = nxt('stgb', 2)
                    act(stgb[sbi][:], g[:], AF.Copy, [('stg', s)], [('stgb', sbi)])
                    for a in range(4):
                        tp(pt[0][:, a * 128:(a + 1) * 128], stgb[sbi][:, a * 128:(a + 1) * 128], identb[:], [('stgb', sbi), 'identb'], [('pt', 0)])
                    srcv = pt[0][:, 0:512].rearrange("p (a t) -> p a t", a=4)
                    if which == 'q':
                        cp('dve', QB[:, :, i * 128:(i + 1) * 128], srcv, [('pt', 0)], ['QB'])
                    else:
                        cp('dve', kTst[:, :, i * 128:(i + 1) * 128], srcv, [('pt', 0)], ['kTst'])
                return f

            def v_evac(i, ps, pkey):
                act(vst[:, i, :], ps, AF.Copy, [pkey], ['vst'])

            proj_tok(B_Q, 512, qk_evac('q'))
            proj_tok(B_K, 512, qk_evac('k'))
            dma('sp', c1i[0:512, :].rearrange("(a p) t -> p a t", p=128), kTst, ['kTst'], ['c1i'], 'c1w0')
            proj_tok(B_V, 512, v_evac)
            dma('sp', c1i[1024:1536, :].rearrange("r (two c) -> (r two) c", two=2).rearrange("(i p) c -> p i c", p=128), vst, ['vst'], ['c1i'], 'c1w1')
            proj_feat(C_Q, lambda c, hf, ps, pkey: act(QC[:, c, hf * 512:(hf + 1) * 512], ps, AF.Copy, [pkey], ['QC']))
            proj_feat(C_K, lambda c, hf, ps, pkey: act(kTst[:, c, hf * 512:(hf + 1) * 512], ps, AF.Copy, [pkey], ['kTst']))
            dma('sp', c1i[512:1024, :].rearrange("(a p) t -> p a t", p=128), kTst, ['kTst'], ['c1i'], 'c1w0')
            proj_tok(C_V, 512, v_evac)
            dma('sp', c1i[1536:2048, :].rearrange("r (two c) -> (r two) c", two=2).rearrange("(i p) c -> p i c", p=128), vst, ['vst'], ['c1i'], 'c1w1')
            if not nocc:
                for r in range(8):
                    S.op('pool', lambda e, r=r: e.collective_compute("AllGather", ALU.bypass, replica_groups=RG,
                                                                     ins=[c1i[r * 256:(r + 1) * 256, :].opt()], outs=[c1oc[r].opt()]),
                         reads=['c1i'], writes=[('c1o', r)], dma=('cc1', r), inc=1)
            dma('pool', wf[:], w_in_l[:, C_F:C_F + 4].rearrange("(k p) c -> p k c", p=128), [], ['wf'], 'wf')
            for i in range(NT):
                p = nxt('pa', 6)
                for k in range(KC):
                    mm(pa[p][:, 0:4], xnT[:, k, i * 128:(i + 1) * 128], wf[:, k, :], k == 0, k == KC - 1, [('dstT', i), 'wf'], [('pa', p)])
                tt('dve', nlst[:, i, :], pa[p][:, 0:4], bfb[:], ALU.add, [('pa', p), 'bfb'], ['nlst'])
            act(nlst, nlst, AF.Exp, ['nlst'], ['nlst'], scale=-1.0)
            act(nlst, nlst, AF.Ln, ['nlst'], ['nlst'], bias=1.0)
            dma('sp', c2i[0:4, :].rearrange("r (t h) -> (r t) h", h=4).rearrange("(i p) h -> p i h", p=128), nlst, ['nlst'], ['c2i'], 'c2w0')
            proj_feat(D_H, lambda c, hf, ps, pkey: act(hD[:, c, 16 + hf * 512:16 + (hf + 1) * 512], ps, AF.Copy, [pkey], ['hD']))
            dma('sp', c2i[4:12, :].rearrange("r (c t) -> (r c) t", t=16).rearrange("(g c) t -> c g t", c=128), hD[:, :, 1024:1040], ['hD'], ['c2i'], 'c2w1')
            if not nocc:
                S.op('pool', lambda e: e.collective_compute("AllGather", ALU.bypass, replica_groups=RG, ins=[c2i.opt()], outs=[c2o.opt()]),
                     reads=['c2i'], writes=['c2o'], dma='cc2', inc=1)

            if stop < 2 or (mode == 'B2' and l == 1):
                break
            barrier()
            uT = X[:, 0:4096].rearrange("p (g t) -> p g t", g=4)
            vn = X[:, 4096:8192].rearrange("p (i c) -> p i c", i=NT)
            lng = Xf[:, 4096:4608]
            lnb = Xf[:, 4608:5120]
            bsb = Xf[:, 5120:5632].rearrange("p (g t) -> p g t", g=4)
            wsf = Xf[:, 5632:6144].rearrange("p (g t) -> p g t", g=4)
            WcT = X[:, 12288:12800].rearrange("p (g t) -> p g t", g=4)
            tmpA = Xf[:, 6400:6912].rearrange("p (g t) -> p g t", g=4)
            dma('sp', lng, gm_ln_g[lw].partition_broadcast(128), [], ['lng'], 'lng')
            dma('sp', lnb, gm_ln_b[lw].partition_broadcast(128), [], ['lnb'], 'lnb')
            dma('sp', bsb, gm_b_s[lw].rearrange("g t -> (g t)").partition_broadcast(128).rearrange("p (g t) -> p g t", g=4), [], ['bsb'], 'bsb')
            dma('sp', wsf, gm_w_s[lw].rearrange("g t s -> t g s"), [], ['wsf'], 'wsf')
            for g in range(4):
                tt('dve', wsf[:, g, :], wsf[:, g, :], tril_f, ALU.mult, ['wsf', 'cst'], ['wsf'])
            pw = nxt('pa', 6)
            for g in range(4):
                tp(pa[pw][:, g * 128:(g + 1) * 128], wsf[:, g, :], ident_f, ['wsf', 'cst'], [('pa', pw)])
            cp('dve', WcT, pa[pw][:, :].rearrange("p (g t) -> p g t", g=4), [('pa', pw)], ['WcT'])
            proj_feat(A_U, lambda c, hf, ps, pkey: act(uT[:, c, hf * 512:(hf + 1) * 512], ps, AF.Copy, [pkey], ['uT']))

            def av_evac(i, ps, pkey):
                s = nxt('stg', 2)
                sbi = nxt('stgb', 2)
                act(stgb[sbi][:], ps, AF.Square, [pkey], [('stgb', sbi), 'stat'], accum=stat[:, 8:9])
                S.op('dve', lambda e: e.reduce_sum(out=stat[:, 9:10], in_=ps, axis=AX.X), reads=[pkey], writes=['stat'])
                ts('dve', stat[:, 9:10], stat[:, 9:10], 1.0 / 512, None, ALU.mult, None, ['stat'], ['stat'])
                tt('dve', stat[:, 10:11], stat[:, 9:10], stat[:, 9:10], ALU.mult, ['stat'], ['stat'])
                stt('dve', stat[:, 11:12], stat[:, 8:9], 1.0 / 512, stat[:, 10:11], ALU.mult, ALU.subtract, ['stat'], ['stat'])
                rstd_from(stat[:, 12:13], stat[:, 11:12], 1.0, ['stat'], ['stat'])
                stt('dve', stat[:, 13:14], stat[:, 9:10], -1.0, stat[:, 12:13], ALU.mult, ALU.mult, ['stat'], ['stat'])
                act(stg[s][:], ps, AF.Identity, [pkey, 'stat'], [('stg', s)], bias=stat[:, 13:14], scale=stat[:, 12:13])
                tt('dve', stg[s][:], stg[s][:], lng, ALU.mult, [('stg', s), 'lng'], [('stg', s)])
                tt('dve', vn[:, i, :], stg[s][:], lnb, ALU.add, [('stg', s), 'lnb'], [('vn', i)])

            proj_tok(A_V, 512, av_evac)
            for i in range(NT):
                p = nxt('pa', 6)
                for g in range(4):
                    mm(pa[p][:, g * 128:(g + 1) * 128], vn[:, i, g * 128:(g + 1) * 128], WcT[:, g, :], True, True, [('vn', i), 'WcT'], [('pa', p)])
                tt('dve', tmpA, pa[p][:, :].rearrange("p (g t) -> p g t", g=4), bsb, ALU.add, [('pa', p), 'bsb'], ['tmpA'])
                tt('dve', oT[:, 0:4, i * 128:(i + 1) * 128], tmpA, uT[:, :, i * 128:(i + 1) * 128], ALU.mult, ['tmpA', 'uT'], [('oT', 0)])
            if l == dbg_layer:
                dump('oA', oT[:, 0, :], [('oT', 0)])

            if stop < 3:
                break
            barrier()
            halo = Xf[:, 0:64].rearrange("p (g t) -> p g t", g=4)
            ta = Xf[:, 64:1104]
            tb = Xf[:, 1104:2144]
            pT = X[:, 8192:9216]
            pwb = X[:, 9216:9728].rearrange("p (g d) -> p g d", g=4)
            dma('sp', halo, c2o[4:12, :].rearrange("r (c t) -> (r c) t", t=16).rearrange("(g c) t -> c g t", c=128), ['c2o'], ['halo'], 'halo')
            dma('pool', pwb, pool_w[lw].rearrange("g c d -> c g d"), [], ['pwb'], 'pwb')
            ts('dve', hD[:, :, 0:16], halo, flag, None, ALU.mult, None, ['halo', 'cfgs', 'hD'], ['hD'])
            for g in range(4):
                a = hD[:, g, :]
                cur = a
                bufs = [ta, tb]
                for step in range(g + 1):
                    sh = 1 << step
                    dst = bufs[step % 2]
                    eng = 'dve' if step % 2 == 0 else 'pool'
                    tt(eng, dst[:, sh:1040], cur[:, sh:1040], cur[:, 0:1040 - sh], ALU.add, ['hD', 'ta', 'tb'], ['ta' if step % 2 == 0 else 'tb'])
                    cur = dst
                wdt = 2 << g
                stt('dve', pT[:, :], cur[:, 16:1040], 1.0 / wdt, a[:, 16:1040], ALU.mult, ALU.subtract, ['ta', 'tb', 'hD'], ['pT'])
                tt('dve', halo[:, g, :], cur[:, 16:32], rcnt[:, g, :], ALU.mult, ['ta', 'tb', 'cfgs'], ['halo'])
                tt('dve', pT[:, 0:16], halo[:, g, :], a[:, 16:32], ALU.subtract, ['halo', 'hD'], ['pT'])
                for hf in range(2):
                    p = nxt('pa', 6)
                    mm(pa[p][:], pwb[:, g, :], pT[:, hf * 512:(hf + 1) * 512], True, True, ['pT', 'pwb'], [('pa', p)])
                    act(oT[:, 12 + g, hf * 512:(hf + 1) * 512], pa[p][:], AF.Copy, [('pa', p), 'pv'], [('oT', 3)], scale=pv[:, g:g + 1])
            if l == dbg_layer:
                dump('oD', oT[:, 12, :], [('oT', 3)])

            if stop < 4:
                break
            barrier()
            kTh = [X[:, 0:2048], X[:, 2048:4096]]
            vh = [X[:, 4096:6144].rearrange("p (j c) -> p j c", j=16), X[:, 6144:8192].rearrange("p (j c) -> p j c", j=16)]
            Pb = [X[:, 8192:8704], X[:, 8704:9216]]
            r1 = Xf[:, 4608:5120]
            r2 = Xf[:, 5120:5632]
            oa = Xf[:, 5632:6144]
            ob = Xf[:, 6144:6656]
            sqb = X[:, 13312:13824]
            nl_all = Xf[:, 7168:7232].rearrange("p (j h) -> p j h", j=16)
            cum = Xf[:, 7232:7296].rearrange("p (j h) -> p j h", j=16)
            pref = Xf[:, 7296:7360].rearrange("p (j h) -> p j h", j=16)
            tot = Xf[:, 7360:7424].rearrange("p (j h) -> p j h", j=16)
            crefn = Xf[:, 7424:7456].rearrange("p (q h) -> p q h", q=8)
            btab = Xf[:, 7456:7968].rearrange("p (h j q) -> p h j q", h=4, j=16)

            def load_kv(hd, krow0, vrow0, slot):
                kr = krow0 + hd * 128
                dma('sp', kTh[slot][:, 0:1024], c1oc[kr // 256][kr % 256:kr % 256 + 128, :], [('c1o', kr // 256)], [('kTh', slot)], ('kTh', slot))
                dma('sp', kTh[slot][:, 1024:2048], c1i[kr:kr + 128, :], ['c1i'], [('kTh', slot)], ('kTh', slot))
                for hh in range(2):
                    vsrc = c1oc[vrow0 // 256 + hh][0:256, :].rearrange("r (two c) -> (r two) c", two=2)[:, hd * 128:(hd + 1) * 128]
                    dma('sp', vh[slot][:, hh * 4:hh * 4 + 4, :], vsrc.rearrange("(j q) c -> q j c", q=128), [('c1o', vrow0 // 256 + hh)], [('vh', slot)], ('vh', slot))
                vsrc = c1i[vrow0:vrow0 + 512, :].rearrange("r (two c) -> (r two) c", two=2)[:, hd * 128:(hd + 1) * 128]
                dma('sp', vh[slot][:, 8:16, :], vsrc.rearrange("(j q) c -> q j c", q=128), ['c1i'], [('vh', slot)], ('vh', slot))

            def attention(nmaps, Q, krow0, vrow0, finalize, bias_fn, scale):
                dk = 128 // nmaps
                for hd in range(4):
                    slot = hd % 2
                    load_kv(hd, krow0, vrow0, slot)
                    for qc in range(2):
                        nj = 8 + 4 * qc + 4
                        for j in range(nj):
                            lo = 0 if j < 8 + 4 * qc else (j - 8 - 4 * qc) * 128
                            diag = j >= 8 + 4 * qc
                            for m in range(nmaps):
                                psS = 4 + nxt('pa', 2)
                                mm(pa[psS][:, lo:512], kTh[slot][m * dk:(m + 1) * dk, j * 128:(j + 1) * 128],
                                   Q[m * dk:(m + 1) * dk, hd, qc * 512 + lo:(qc + 1) * 512], True, True,
                                   [('kTh', slot)], [('pa', psS)])
                                pbi = nxt('stgb', 2)
                                P = Pb[pbi]
                                bias_fn(P, pa[psS], ('pa', psS), ('Pb', pbi), hd, j, qc, lo, scale)
                                if diag:
                                    tt('pool', P[:, lo:lo + 128], P[:, lo:lo + 128], maskb[:], ALU.mult, [('Pb', pbi), 'maskb'], [('Pb', pbi)])
                                mm(pa[m][:, lo:512], vh[slot][:, j, :], P[:, lo:512], j == 0, j == nj - 1, [('vh', slot), ('Pb', pbi)], [('pa', m)])
                                mm(pa[2 + m][:, lo:512], onesb[:], P[:, lo:512], j == 0, j == nj - 1, ['onesb', ('Pb', pbi)], [('pa', 2 + m)])
                        finalize(hd, qc)

            def bias_B(P, ps, pkey, Pkey, hd, j, qc, lo, scale):
                if j < 8:
                    act(P[:, lo:512], ps[:, lo:512], AF.Exp, [pkey, 'cfgs'], [Pkey], bias=prevb, scale=scale)
                else:
                    act(P[:, lo:512], ps[:, lo:512], AF.Exp, [pkey], [Pkey], scale=scale)

            def fin_B(hd, qc):
                S.op('dve', lambda e: e.reciprocal(out=r1, in_=pa[2][:]), reads=[('pa', 2)], writes=['r1'])
                S.op('dve', lambda e: e.reciprocal(out=r2, in_=pa[3][:]), reads=[('pa', 3)], writes=['r2'])
                tt('dve', oa, pa[0][:], r1, ALU.mult, [('pa', 0), 'r1'], ['oa'])
                tt('dve', ob, pa[1][:], r2, ALU.mult, [('pa', 1), 'r2'], ['ob'])
                stt('dve', oa, ob, neglam, oa, ALU.mult, ALU.add, ['ob', 'oa', 'lam'], ['oa'])
                act(sqb, oa, AF.Square, ['oa'], ['sqb'])
                mm(pa[2][:], onesb[:], sqb, True, True, ['onesb', 'sqb'], [('pa', 2)])
                rstd_from(r1, pa[2][:], 1.0 / 128, [('pa', 2)], ['r1'])
                tt('dve', oa, oa, r1, ALU.mult, ['oa', 'r1'], ['oa'])
                act(oT[:, 4 + hd, qc * 512:(qc + 1) * 512], oa, AF.Copy, ['oa', 'lam'], [('oT', 1)], scale=subg)

            attention(2, QB, 0, 1024, fin_B, bias_B, 64 ** -0.5)
            if l == dbg_layer:
                dump('oB', oT[:, 4, :], [('oT', 1)])

            if stop < 5:
                break
            barrier()
            dma('sp', nl_all[:, 0:8, :], c2o[0:4, :].rearrange("r (t h) -> (r t) h", h=4).rearrange("(i p) h -> p i h", p=128), ['c2o'], ['nl_all'], 'nla')
            dma('sp', nl_all[:, 8:16, :], c2i[0:4, :].rearrange("r (t h) -> (r t) h", h=4).rearrange("(i p) h -> p i h", p=128), ['c2i'], ['nl_all'], 'nla')
            nlf = Xf[:, 7168:7232]
            p = nxt('pa', 4)
            mm(pa[p][:, 0:64], ones_f, nlf, True, True, ['cst', 'nl_all'], [('pa', p)])
            cp('dve', tot, pa[p][:, 0:64].rearrange("p (j h) -> p j h", j=16), [('pa', p)], ['tot'])
            S.op('dve', lambda e: e.memset(pref[:, 0, :], 0.0), reads=[], writes=['pref'])
            for j in range(1, 16):
                tt('dve', pref[:, j, :], pref[:, j - 1, :], tot[:, j - 1, :], ALU.add, ['pref', 'tot'], ['pref'])
            p = nxt('pa', 4)
            mm(pa[p][:, 0:64], triu_f, nlf, True, True, ['cst', 'nl_all'], [('pa', p)])
            tt('dve', cum, pa[p][:, 0:64].rearrange("p (j h) -> p j h", j=16), pref, ALU.add, [('pa', p), 'pref'], ['cum'])
            p = nxt('pa', 4)
            mm(pa[p][:, 0:64], half_f, nlf, True, True, ['cst', 'nl_all'], [('pa', p)])
            stt('dve', crefn, pa[p][:, 32:64].rearrange("p (q h) -> p q h", q=8), -1.0, pref[:, 8:16, :], ALU.mult, ALU.subtract,
                [('pa', p), 'pref'], ['crefn'])
            for hd in range(4):
                for j in range(16):
                    if j < 8:
                        ts('dve', btab[:, hd, j, :], crefn[:, :, hd], cum[:, j, hd:hd + 1], prevb, ALU.add, ALU.add, ['crefn', 'cum', 'cfgs'], ['btab'])
                    else:
                        ts('dve', btab[:, hd, j, :], crefn[:, :, hd], cum[:, j, hd:hd + 1], None, ALU.add, None, ['crefn', 'cum'], ['btab'])

            def bias_C(P, ps, pkey, Pkey, hd, j, qc, lo, scale):
                for qt in range(lo // 128, 4):
                    act(P[:, qt * 128:(qt + 1) * 128], ps[:, qt * 128:(qt + 1) * 128], AF.Exp, [pkey, 'btab'], [Pkey],
                        bias=btab[:, hd, j, 4 * qc + qt:4 * qc + qt + 1], scale=scale)

            def fin_C(hd, qc):
                S.op('dve', lambda e: e.reciprocal(out=r1, in_=pa[2][:]), reads=[('pa', 2)], writes=['r1'])
                tt('dve', oT[:, 8 + hd, qc * 512:(qc + 1) * 512], pa[0][:], r1, ALU.mult, [('pa', 0), 'r1'], [('oT', 2)])

            attention(1, QC, 512, 1536, fin_C, bias_C, 128 ** -0.5)
            if l == dbg_layer:
                dump('oC', oT[:, 8, :], [('oT', 2)])

            if stop < 6:
                break
            barrier()
            acc = Xf[:, 0:4096].rearrange("p (c t) -> p c t", c=4)
            for dg in range(4):
                for n in range(4):
                    b = load_w(w_in_l[:, GATE + n * D + dg * 512:GATE + n * D + (dg + 1) * 512], 512)
                    wb_i = nxt('wbr', 2)
                    dma('pool', wbr[wb_i][:], w_branch[lw, n][:, dg * 512:(dg + 1) * 512].rearrange("(k p) c -> p k c", p=128), [], [('wbr', wb_i)],
                        ('wbr', wb_i), nobarrier=True)
                    for dc in range(4):
                        for hf in range(2):
                            pg = nxt('pa', 6)
                            for k in range(KC):
                                mm(pa[pg][:], wbuf[b][:, k, dc * 128:(dc + 1) * 128], xnT[:, k, hf * 512:(hf + 1) * 512], k == 0, k == KC - 1,
                                   [('w', b)], [('pa', pg)])
                            pb = nxt('pa', 6)
                            for cc in range(4):
                                mm(pa[pb][:], wbr[wb_i][:, cc, dc * 128:(dc + 1) * 128], oT[:, n * 4 + cc, hf * 512:(hf + 1) * 512], cc == 0, cc == 3,
                                   [('wbr', wb_i), ('oT', n)], [('pa', pb)])
                            s = nxt('stg', 2)
                            act(stg[s][:], pa[pg][:], AF.Sigmoid, [('pa', pg)], [('stg', s)])
                            av = acc[:, dc, hf * 512:(hf + 1) * 512]
                            akey = ('acc', dc, hf)
                            if n == 0:
                                tt('dve', av, pa[pb][:], stg[s][:], ALU.mult, [('pa', pb), ('stg', s)], [akey])
                            else:
                                tt('dve', stg[s][:], pa[pb][:], stg[s][:], ALU.mult, [('pa', pb), ('stg', s)], [('stg', s)])
                                if n < 3:
                                    tt('dve', av, av, stg[s][:], ALU.add, [akey, ('stg', s)], [akey])
                                else:
                                    tt('dve', merged[:, dg * 4 + dc, hf * 512:(hf + 1) * 512], av, stg[s][:], ALU.add, [akey, ('stg', s)],
                                       [('dstT', i) for i in range(hf * 4, hf * 4 + 4)])
            if l == dbg_layer:
                dump('merged', merged[:, 0, :], [('dstT', i) for i in range(NT)])

            if stop < 7:
                break
            barrier()
            dma('sp', gbc[:], norm_mix_post[lw].partition_broadcast(128), [], ['gbc'], 'gbc')

            def wo_evac(cg):
                def f(i, ps, pkey):
                    act(yv[:, i, cg * 512:(cg + 1) * 512], ps, AF.Copy, [pkey], [('y', i)])
                    sbi = nxt('stgb', 2)
                    act(stgb[sbi][:], ps, AF.Square, [pkey], [('stgb', sbi), ('ssq', i)], accum=ssq[:, i, cg:cg + 1])
                return f

            for cg in range(4):
                proj_tok(0, 512, wo_evac(cg), wsrc=w_out[lw][:, cg * 512:(cg + 1) * 512], lhs=merged)

            def residual(dst_dram, src_dram, dst2=None):
                for i in range(NT):
                    hb = nxt('ht', 2)
                    ht = htb[hb]
                    dma('sp', ht, src_dram[i * 128:(i + 1) * 128, :], [('hbuf', i)], [('ht', hb)], ('ht', hb))
                    S.op('dve', lambda e, i=i: e.reduce_sum(out=stat[:, 16:17], in_=ssq[:, i, :], axis=AX.X), reads=[('ssq', i)], writes=['stat'])
                    rstd_from(stat[:, 17:18], stat[:, 16:17], 1.0 / D, ['stat'], ['stat'])
                    stt('dve', yv[:, i, :], yv[:, i, :], stat[:, 17:18], gbc[:], ALU.mult, ALU.mult, [('y', i), 'stat', 'gbc'], [('y', i)])
                    tt('dve', ht, ht, yv[:, i, :], ALU.add, [('ht', hb), ('y', i)], [('ht', hb)])
                    dma('sp', dst_dram[i * 128:(i + 1) * 128, :], ht, [('ht', hb)], [('hbuf', i)], ('hto', hb))
                    if dst2 is not None:
                        dma('sp', dst2[i * 128:(i + 1) * 128, :], ht, [('ht', hb)], [('hbuf2', i)], ('hto2', hb))

            residual(hbuf, hsrc)
            if l == dbg_layer:
                barrier()
                dump('h1', hbuf[0:128, :], [('hbuf', 0)])

            if stop < 8:
                break
            barrier()
            norm_transpose(hbuf, norm_ffn_pre[lw], hnT)
            barrier()
            ffT = [X[:, 0:8192].rearrange("p (c t) -> p c t", c=8), X[:, 8192:16384].rearrange("p (c t) -> p c t", c=8)]
            for fb in range(8):
                fs = fb % 2
                for sub in range(2):
                    def up_evac(c, hf, ps, pkey, sub=sub, fs=fs):
                        s = nxt('stg', 2)
                        act(stg[s][:], ps, AF.Relu, [pkey], [('stg', s)])
                        tt('dve', ffT[fs][:, sub * 4 + c, hf * 512:(hf + 1) * 512], stg[s][:], stg[s][:], ALU.mult, [('stg', s)],
                           [('ffT', fs, i) for i in range(hf * 4, hf * 4 + 4)])
                    proj_feat(0, up_evac, wsrc=w_ffn_up[lw][:, (fb * 2 + sub) * 512:(fb * 2 + sub + 1) * 512], rhsT=hnT)
                for cg in range(4):
                    b = load_w(w_ffn_down[lw][fb * 1024:(fb + 1) * 1024, cg * 512:(cg + 1) * 512], 512)
                    for i in range(NT):
                        p = nxt('pa', 6)
                        for fc in range(8):
                            mm(pa[p][:], ffT[fs][:, fc, i * 128:(i + 1) * 128], wbuf[b][:, fc, :], fc == 0, fc == 7, [('ffT', fs, i), ('w', b)], [('pa', p)])
                        yy = yv[:, i, cg * 512:(cg + 1) * 512]
                        if fb == 0:
                            act(yy, pa[p][:], AF.Copy, [('pa', p)], [('y', i)])
                        else:
                            tt('dve', yy, pa[p][:], yy, ALU.add, [('pa', p), ('y', i)], [('y', i)])
            barrier()
            dma('sp', gbc[:], norm_ffn_post[lw].partition_broadcast(128), [], ['gbc'], 'gbc')
            for i in range(NT):
                act(xb, yv[:, i, 0:D], AF.Square, [('y', i)], ['xb', ('ssq', i)], accum=ssq[:, i, 0:1])
                S.op('dve', lambda e, i=i: e.memset(ssq[:, i, 1:4], 0.0), reads=[], writes=[('ssq', i)])
            residual(out if l == depth - 1 else hbuf, hbuf, dst2=(out if (mode == 'B2' and l == 0) else None))
            if l == dbg_layer and l < depth - 1:
                barrier()
                dump('hend', hbuf[0:128, :], [('hbuf', 0)])
                dump('yend', yv[:, 0, :], [('y', 0)])

        S.op('sp', None, reads=[('hbuf', i) for i in range(NT)] + [('hbuf2', i) for i in range(NT)], writes=[('hbuf', i) for i in range(NT)] + [('hbuf2', i) for i in range(NT)] + ['PHASE'])
        S.emit(block, st)
        build.stats = (len(S.ops), dict(S.sem_counts))
    return nc


def host_inputs(inputs, depth=2):
    f32 = np.float32
    x = np.ascontiguousarray(np.asarray(inputs["x"], dtype=f32))
    positions = np.asarray(inputs["positions"]).astype(np.int32)
    consts = np.zeros((128, 5, 128), f32)
    r = np.arange(128)
    consts[:, 0, :] = np.eye(128, dtype=f32)
    consts[:, 1, :] = (r[None, :] >= r[:, None]).astype(f32)
    consts[:, 2, :] = (r[None, :] <= r[:, None]).astype(f32)
    consts[:, 3, :] = 1.0
    consts[:, 4, :] = (r[:, None] <= 63).astype(f32)
    inv = (1.0 / (np.float32(500000.0) ** (np.arange(0, 16, 2, dtype=f32) / np.float32(16)))).astype(f32)
    pvec = np.zeros((2, 128, 5), f32)
    ps = np.asarray(inputs["pool_scale"], dtype=f32).reshape(2, 4, 128)
    pvec[:, :, 0:4] = ps.transpose(0, 2, 1)
    pvec[:, :, 4] = np.asarray(inputs["da_subln_g"], dtype=f32)
    shared = {k: np.ascontiguousarray(np.asarray(inputs[k], dtype=f32)) for k in
              ["norm_mix_pre", "norm_mix_post", "norm_ffn_pre", "norm_ffn_post", "w_in", "gm_ln_g", "gm_ln_b", "gm_w_s", "gm_b_s",
               "da_lambda", "fa_b_f", "pool_w", "w_branch", "w_out", "w_ffn_up", "w_ffn_down"]}
    shared["pvec"] = pvec
    shared["consts"] = consts
    in_maps = []
    for c in range(8):
        b, half = c // 2, c % 2
        cfg = np.zeros((128, 74), f32)
        cfg[:, 0] = 0.0 if half == 1 else NEG
        cfg[:, 1] = 1.0 if half == 1 else 0.0
        for g in range(4):
            w = 2 << g
            t = np.arange(16)
            cnt = np.minimum(t + 1, w) if half == 0 else np.full(16, w)
            cfg[:, 2 + g * 16:2 + (g + 1) * 16] = (1.0 / cnt.astype(f32))[None, :]
        cfg[:, 66:74] = inv[None, :]
        m = dict(shared)
        m["x"] = np.ascontiguousarray(x[b, half * 1024:(half + 1) * 1024, :])
        m["pos"] = np.ascontiguousarray(positions[b, half * 1024:(half + 1) * 1024].reshape(NT, 128).T)
        m["cfg"] = cfg
        in_maps.append(m)
    return in_maps


_NC_CACHE = {}


def _get(mode, lb):
    key = (mode, lb)
    if key not in _NC_CACHE:
        _NC_CACHE[key] = build(1, mode=mode, layer_base=lb)
    return _NC_CACHE[key]


def kernel(**inputs):
    in_maps = host_inputs(inputs)
    cores = list(range(8))
    resA = run_bass_kernel_spmd(_get('A', 0), in_maps, core_ids=cores)
    prev = resA.results
    for l in range(2):
        mapsB = []
        for c in range(8):
            e = c - c % 2
            m = dict(in_maps[c])
            m["prev1"] = np.ascontiguousarray(prev[e]["c1i_out"])
            m["prev2"] = np.ascontiguousarray(np.concatenate([prev[e]["c2i_out"], prev[e + 1]["c2i_out"]], axis=0))
            mapsB.append(m)
        resB = run_bass_kernel_spmd(_get('B2' if l == 0 else 'B', l), mapsB, core_ids=cores)
        prev = resB.results
        for c in range(8):
            in_maps[c] = dict(in_maps[c])
            in_maps[c]["x"] = np.ascontiguousarray(resB.results[c]["out"])
    outp = np.zeros((4, 2048, 2048), np.float32)
    for c in range(8):
        b, half = c // 2, c % 2
        outp[b, half * 1024:(half + 1) * 1024, :] = in_maps[c]["x"]
    return outp
```
